# Optimizing a Trainium2 kernel written in Bass

```python
import jax
import jax.numpy as jnp
from jax import lax
import numpy as np

D_MODEL = 1024
BATCH = 1
SEQ = 16384
DEPTH = 2

GRID_W = 64
CTX_LEN = 256
N_MOD = 9
D_FF = 2816
NA_HEADS = 8
NA_HEAD_DIM = 64
NA_WIDTH = NA_HEADS * NA_HEAD_DIM
NA_KH = 8
NA_KW = 16
POOL_WINDOWS = (2, 4, 8, 16)
POOL_GROUPS = len(POOL_WINDOWS)
POOL_WIDTH = D_MODEL - NA_WIDTH
POOL_GROUP_DIM = POOL_WIDTH // POOL_GROUPS
EVEN_IN_WIDTH = 3 * NA_WIDTH + POOL_WIDTH
EVEN_MIX_WIDTH = NA_WIDTH + POOL_WIDTH
CONV_WIDTH = D_MODEL
CONV_K = 3
N_EVEN = (DEPTH + 1) // 2
N_ODD = DEPTH // 2
RMS_EPS = 1e-6
NEG_INF = -1e30

kernel_name = "hybrid_natten_pool_shortconv_dit"


def rms_norm(x, g):
    x32 = x.astype(jnp.float32)
    y = x32 * lax.rsqrt(jnp.mean(x32 * x32, axis=-1, keepdims=True) + RMS_EPS)
    return (y * g.astype(jnp.float32)).astype(x.dtype)


def modulate(h, shift, scale):
    return h * (1 + scale) + shift


def adaln(cond, w, b):
    m = jax.nn.silu(cond) @ w + b
    return m.reshape(m.shape[:-1] + (N_MOD, D_MODEL))


def swiglu(h, w13, w2):
    a, b = jnp.split(h @ w13, 2, axis=-1)
    return (jax.nn.silu(a) * b) @ w2


def ffn_sublayer(h, m, g, w13, w2, base):
    hn = modulate(rms_norm(h, g), m[:, :, base], m[:, :, base + 1])
    return h + 0.5 * m[:, :, base + 2] * swiglu(hn, w13, w2)


def split_heads(t):
    return t.reshape(t.shape[:2] + (NA_HEADS, NA_HEAD_DIM))


def neighbourhood_attention(q, k, v, k_ctx, v_ctx, rpb):
    B, L, H, dh = q.shape
    R = L // GRID_W
    kh = min(NA_KH, R)
    kw = NA_KW
    scale = dh ** -0.5
    qg = q.reshape(B, R, GRID_W, H, dh)
    kg = k.reshape(B, R, GRID_W, H, dh)
    vg = v.reshape(B, R, GRID_W, H, dh)
    r = jnp.arange(R)
    row_start = jnp.clip(r - kh // 2, 0, R - kh)
    ridx = row_start[:, None] + jnp.arange(kh)[None, :]
    k_blk = kg[:, ridx]
    v_blk = vg[:, ridx]
    col = jnp.arange(GRID_W)
    col_start = jnp.clip(col - kw // 2, 0, GRID_W - kw)
    col_ok = (col[None, :] >= col_start[:, None]) & (col[None, :] < col_start[:, None] + kw)
    ri = ridx - r[:, None] + (NA_KH - 1)
    ci = jnp.clip(col[None, :] - col[:, None] + (NA_KW - 1), 0, 2 * NA_KW - 2)
    bias = rpb[:, ri[:, None, :, None], ci[None, :, None, :]].astype(jnp.float32)
    s_win = jnp.einsum('brqhd,brkwhd->bhrqkw', qg, k_blk).astype(jnp.float32) * scale + bias
    s_win = jnp.where(col_ok[:, None, :], s_win, NEG_INF)
    n_win = kh * GRID_W
    s_win = s_win.reshape(B, H, R, GRID_W, n_win)
    s_ctx = jnp.einsum('brqhd,bchd->bhrqc', qg, k_ctx).astype(jnp.float32) * scale
    p = jax.nn.softmax(jnp.concatenate([s_win, s_ctx], axis=-1), axis=-1).astype(v.dtype)
    p_win = p[..., :n_win].reshape(B, H, R, GRID_W, kh, GRID_W)
    o = (jnp.einsum('bhrqkw,brkwhd->brqhd', p_win, v_blk)
         + jnp.einsum('bhrqc,bchd->brqhd', p[..., n_win:], v_ctx))
    return o.reshape(B, L, H * dh)


def context_attention(q, k, v):
    B, C, H, dh = q.shape
    s = jnp.einsum('bqhd,bkhd->bhqk', q, k).astype(jnp.float32) * dh ** -0.5
    p = jax.nn.softmax(s, axis=-1).astype(v.dtype)
    return jnp.einsum('bhqk,bkhd->bqhd', p, v).reshape(B, C, H * dh)


def multiscale_pool(u, pool_w, pool_scale):
    B, L, _ = u.shape
    t = jnp.arange(L)
    ug = u.reshape(B, L, POOL_GROUPS, POOL_GROUP_DIM)
    outs = []
    for g, w in enumerate(POOL_WINDOWS):
        xg = ug[:, :, g].astype(jnp.float32)
        cs = jnp.pad(jnp.cumsum(xg, axis=1), ((0, 0), (1, 0), (0, 0)))
        lo = jnp.clip(t - w // 2, 0, L)
        hi = jnp.clip(t - w // 2 + w, 0, L)
        cnt = (hi - lo).astype(jnp.float32)[None, :, None]
        mean = (jnp.take(cs, hi, axis=1) - jnp.take(cs, lo, axis=1)) / cnt
        outs.append((mean - xg).astype(u.dtype) @ pool_w[g])
    return jnp.concatenate(outs, axis=-1) * pool_scale


def even_mixer(hl, hc, w_in, w_out, rpb, pool_w, pool_scale, ctx_out):
    q, k, v, u = jnp.split(hl @ w_in, [NA_WIDTH, 2 * NA_WIDTH, 3 * NA_WIDTH], axis=-1)
    k_c, v_c = jnp.split(hc @ w_in[:, NA_WIDTH:3 * NA_WIDTH], 2, axis=-1)
    k_c, v_c = split_heads(k_c), split_heads(v_c)
    att = neighbourhood_attention(split_heads(q), split_heads(k), split_heads(v), k_c, v_c, rpb)
    pool = multiscale_pool(u, pool_w, pool_scale)
    y_lat = jnp.concatenate([att, pool], axis=-1) @ w_out
    y_ctx = None
    if ctx_out:
        q_c = split_heads(hc @ w_in[:, :NA_WIDTH])
        u_c = hc @ w_in[:, 3 * NA_WIDTH:]
        att_c = context_attention(q_c, k_c, v_c)
        pool_c = multiscale_pool(u_c, pool_w, pool_scale)
        y_ctx = jnp.concatenate([att_c, pool_c], axis=-1) @ w_out
    return y_lat, y_ctx


def short_conv_mixer(h, w_in, conv_w, w_out):
    bg, cg, xin = jnp.split(h @ w_in, 3, axis=-1)
    z = cg * xin
    L = h.shape[1]
    zp = jnp.pad(z, ((0, 0), (1, 1), (0, 0)))
    y = zp[:, 0:L] * conv_w[0] + zp[:, 1:L + 1] * conv_w[1] + zp[:, 2:L + 2] * conv_w[2]
    return (bg * y) @ w_out


def setup_inputs(seed: int = 0) -> dict:
    key = jax.random.key(seed)
    ks = jax.random.split(key, 18)
    D = D_MODEL

    def nrm(k, shape, s):
        return jax.random.normal(k, shape, jnp.float32) * s

    return {
        "x": nrm(ks[0], (BATCH, SEQ, D), 1.0),
        "c": nrm(ks[1], (BATCH, D), 1.0),
        "ctx": nrm(ks[2], (BATCH, CTX_LEN, D), 1.0),
        "c_ctx": nrm(ks[3], (D,), 1.0),
        "mod_w": nrm(ks[4], (DEPTH, D, N_MOD * D), 0.5 * D ** -0.5),
        "mod_b": nrm(ks[5], (DEPTH, N_MOD * D), 0.01),
        "norm_g": 1.0 + nrm(ks[6], (DEPTH, 3, D), 0.02),
        "ffn_w13": nrm(ks[7], (DEPTH, 2, D, 2 * D_FF), D ** -0.5),
        "ffn_w2": nrm(ks[8], (DEPTH, 2, D_FF, D), D_FF ** -0.5),
        "even_w_in": nrm(ks[9], (N_EVEN, D, EVEN_IN_WIDTH), D ** -0.5),
        "even_w_out": nrm(ks[10], (N_EVEN, EVEN_MIX_WIDTH, D), EVEN_MIX_WIDTH ** -0.5),
        "na_rpb": nrm(ks[11], (N_EVEN, NA_HEADS, 2 * NA_KH - 1, 2 * NA_KW - 1), 0.1),
        "pool_w": nrm(ks[12], (N_EVEN, POOL_GROUPS, POOL_GROUP_DIM, POOL_GROUP_DIM), POOL_GROUP_DIM ** -0.5),
        "pool_scale": 1.0 + nrm(ks[13], (N_EVEN, POOL_WIDTH), 0.1),
        "conv_w_in": nrm(ks[14], (N_ODD, D, 3 * CONV_WIDTH), D ** -0.5),
        "conv_w": nrm(ks[15], (N_ODD, CONV_K, CONV_WIDTH), CONV_K ** -0.5),
        "conv_w_out": nrm(ks[16], (N_ODD, CONV_WIDTH, D), CONV_WIDTH ** -0.5),
        "final_g": 1.0 + nrm(ks[17], (D,), 0.02),
    }


def reference(x, c, ctx, c_ctx, mod_w, mod_b, norm_g, ffn_w13, ffn_w2, even_w_in, even_w_out,
              na_rpb, pool_w, pool_scale, conv_w_in, conv_w, conv_w_out, final_g):
    for i in range(DEPTH):
        even = (i % 2 == 0)
        ctx_out = any(j % 2 == 0 for j in range(i + 1, DEPTH))
        ctx_here = even or ctx_out
        m_l = adaln(c[:, None, :], mod_w[i], mod_b[i])
        m_c = adaln(c_ctx[None, None, :], mod_w[i], mod_b[i])

        x = ffn_sublayer(x, m_l, norm_g[i, 0], ffn_w13[i, 0], ffn_w2[i, 0], 0)
        if ctx_here:
            ctx = ffn_sublayer(ctx, m_c, norm_g[i, 0], ffn_w13[i, 0], ffn_w2[i, 0], 0)

        xn = modulate(rms_norm(x, norm_g[i, 1]), m_l[:, :, 3], m_l[:, :, 4])
        y_c = None
        if even:
            e = i // 2
            cn = modulate(rms_norm(ctx, norm_g[i, 1]), m_c[:, :, 3], m_c[:, :, 4])
            y_l, y_c = even_mixer(xn, cn, even_w_in[e], even_w_out[e], na_rpb[e],
                                  pool_w[e], pool_scale[e], ctx_out)
        else:
            o = i // 2
            y_l = short_conv_mixer(xn, conv_w_in[o], conv_w[o], conv_w_out[o])
            if ctx_out:
                cn = modulate(rms_norm(ctx, norm_g[i, 1]), m_c[:, :, 3], m_c[:, :, 4])
                y_c = short_conv_mixer(cn, conv_w_in[o], conv_w[o], conv_w_out[o])
        x = x + m_l[:, :, 5] * y_l

        x = ffn_sublayer(x, m_l, norm_g[i, 2], ffn_w13[i, 1], ffn_w2[i, 1], 6)
        if ctx_out:
            ctx = ctx + m_c[:, :, 5] * y_c
            ctx = ffn_sublayer(ctx, m_c, norm_g[i, 2], ffn_w13[i, 1], ffn_w2[i, 1], 6)

    return rms_norm(x, final_g)
```

```python
import numpy as np
from contextlib import ExitStack
import concourse.bass as bass
import concourse.mybir as mybir
from concourse.bass_utils import run_bass_kernel_spmd

F32 = mybir.dt.float32
BF16 = mybir.dt.bfloat16
AF = mybir.ActivationFunctionType
ALU = mybir.AluOpType

NCORE = 8
D = 1024
KC = 8
DFF = 2816
GW = 64
NROWS = 256
KVR = 42
KVT = KVR * GW
NPAIR = KVR // 2
EOFF = 4 * GW
ET = 34 * GW
OWNOFF = EOFF + GW
OWNT = 32 * GW
CTXT = 256
NEG = -30000.0
PIPE_LAG = 0
CTX_SHARE = False
POOL_W = (2, 4, 8, 16)

SPEC_UNITS = {0: list(range(0, 7)), 1: list(range(1, 7)), 2: list(range(2, 7)),
              15: list(range(14, 20)), 16: list(range(14, 21))}


def unit_pairs_desc(m):
    if m in SPEC_UNITS:
        return sorted(SPEC_UNITS[m], reverse=True)
    return [m + 4, m + 3, m + 2, m + 1, m]


def tb_index(m, p, a):
    return min(15, max(0, 11 - 2 * (p - m) + a))


def vec_layout():
    lay = {}
    off = 0

    def add(name, n):
        nonlocal off
        lay[name] = off
        off += n

    add("ng", 2 * 3 * 8)
    add("fg", 8)
    add("pscale", 4)
    add("convw", 3 * 8)
    add("modb", 2 * 72)
    add("vm", 2)
    add("icnt", 2 * 4 * 8)
    add("mreg", 4)
    for m in sorted(SPEC_UNITS):
        add(("mspec", m), 2 * len(SPEC_UNITS[m]))
    add("cmask", 64)
    lay["_n"] = off
    return lay


class FW:
    ENGS = ("pe", "act", "dve", "pool", "sp")

    def __init__(self, nc, es):
        self.nc = nc
        self.es = es
        self.sems = {}
        for e in self.ENGS:
            self.sems[e] = es.enter_context(nc.semaphore("sem_" + e))
        self.tick = {e: 0 for e in self.ENGS}
        self.waited = {e: {} for e in self.ENGS}
        self.prog = {e: [] for e in self.ENGS}
        self.res = {}
        self.dma_cnt = {}
        self.n_ins = 0

    def dma_sem(self, key):
        if key not in self.sems:
            self.sems[key] = self.es.enter_context(self.nc.semaphore("sem_" + key))
            self.dma_cnt[key] = 0
        return self.sems[key]

    def _need(self, reads, writes):
        need = {}

        def add(tok):
            if tok is None:
                return
            k, v = tok
            if need.get(k, 0) < v:
                need[k] = v

        for r in reads:
            st = self.res.get(r)
            if st:
                add(st["w"])
        for w in writes:
            st = self.res.get(w)
            if st:
                add(st["w"])
                for t in st["r"]:
                    add(t)
        return need

    def _emit_waits(self, e, need, skip_self=False):
        for k, v in need.items():
            if skip_self and k == e:
                continue
            if self.waited[e].get(k, 0) < v:
                sem = self.sems[k]
                self.prog[e].append(lambda eng, sem=sem, v=v: eng.wait_ge(sem, v))
                self.waited[e][k] = v

    def _record(self, tok, reads, writes):
        for r in reads:
            st = self.res.setdefault(r, {"w": None, "r": []})
            st["r"].append(tok)
            if len(st["r"]) > 24:
                best = {}
                for k, v in st["r"]:
                    if best.get(k, 0) < v:
                        best[k] = v
                st["r"] = list(best.items())
        for w in writes:
            self.res[w] = {"w": tok, "r": []}

    def op(self, e, fn, reads=(), writes=(), inc=True, skip_self=False):
        need = self._need(reads, writes)
        self._emit_waits(e, need, skip_self=skip_self)
        self.n_ins += 1
        if inc:
            sem = self.sems[e]
            self.prog[e].append(lambda eng, fn=fn, sem=sem: fn(eng).then_inc(sem, 1))
            self.tick[e] += 1
            tok = (e, self.tick[e])
        else:
            self.prog[e].append(lambda eng, fn=fn: fn(eng))
            tok = (e, self.tick[e] + 1)
        self._record(tok, reads, writes)
        return tok

    def dma(self, q, semkey, pairs, reads=(), writes=()):
        sem = self.dma_sem(semkey)
        need = self._need(reads, writes)
        self._emit_waits(q, need)
        for (out, in_) in pairs:
            self.dma_cnt[semkey] += 1
            self.prog[q].append(
                lambda eng, out=out, in_=in_, sem=sem: eng.dma_start(out=out, in_=in_).then_inc(sem, 16))
            self.n_ins += 1
        tok = (semkey, 16 * self.dma_cnt[semkey])
        self._record(tok, reads, writes)
        return tok

    def barrier(self, engs=("pe", "act", "dve")):
        snap = {k: self.tick[k] for k in engs}
        for e in engs:
            for k, v in snap.items():
                if k != e and v > 0 and self.waited[e].get(k, 0) < v:
                    sem = self.sems[k]
                    self.prog[e].append(lambda eng, sem=sem, v=v: eng.wait_ge(sem, v))
                    self.waited[e][k] = v

    def emit(self):
        final = {e: self.tick[e] for e in self.ENGS}
        for k, c in self.dma_cnt.items():
            final[k] = 16 * c
        for e in self.ENGS:
            for k, v in final.items():
                if v > 0 and k != e and self.waited[e].get(k, 0) < v:
                    sem = self.sems[k]
                    self.prog[e].append(lambda eng, sem=sem, v=v: eng.wait_ge(sem, v))
        prog = self.prog
        with self.nc.Block() as block:
            @block.tensor
            def _(eng):
                for c in prog["pe"]:
                    c(eng)

            @block.scalar
            def _(eng):
                for c in prog["act"]:
                    c(eng)

            @block.vector
            def _(eng):
                for c in prog["dve"]:
                    c(eng)

            @block.gpsimd
            def _(eng):
                for c in prog["pool"]:
                    c(eng)

            @block.sync
            def _(eng):
                for c in prog["sp"]:
                    c(eng)


def split_tiles(t0, total, tmax=512, align=128):
    out = []
    pos = 0
    while pos < total:
        n = min(tmax, total - pos)
        out.append((t0 + pos, n))
        pos += n
    return out


def build_program(stop_after=None):
    nc = bass.Bass("TRN2", target_bir_lowering=False)
    LAY = vec_layout()
    NV = LAY["_n"]

    def din(name, shape):
        return nc.dram_tensor(name, list(shape), F32, kind="ExternalInput").ap()

    xT = din("xT", [D, KVT])
    ctxT = din("ctxT", [D, CTXT])
    cc_d = din("cc", [128, 16])
    vecs_d = din("vecs", [128, NV])
    tb_d = din("tb", [8, 128, 1152])
    ident_d = din("ident", [128, 128])
    modw = [din("modw%d" % l, [D, 9 * D]) for l in range(2)]
    w13 = [[din("w13_%d%d" % (l, f), [D, 2 * DFF]) for f in range(2)] for l in range(2)]
    w2 = [[din("w2_%d%d" % (l, f), [DFF, D]) for f in range(2)] for l in range(2)]
    w_in = din("win", [D, 2048])
    w_out = din("wout", [D, D])
    poolw = din("poolw", [512, 128])
    cwin = din("cwin", [D, 3 * D])
    cwout = din("cwout", [D, D])
    outT = nc.dram_tensor("outT", [D, OWNT], F32, kind="ExternalOutput").ap()
    if stop_after is not None:
        xdbg = nc.dram_tensor("xdbg", [D, KVT], F32, kind="ExternalOutput").ap()
        mdbg = nc.dram_tensor("mdbg", [128, 288], F32, kind="ExternalOutput").ap()
        kdbg = nc.dram_tensor("kdbg", [128, 4 * KVT], BF16, kind="ExternalOutput").ap()
        vdbg = nc.dram_tensor("vdbg", [128, NPAIR * 512], BF16, kind="ExternalOutput").ap()

    es = ExitStack()
    with es:
        fw = FW(nc, es)

        def sb(name, shape, dt):
            return es.enter_context(nc.sbuf_tensor("sb_" + name, list(shape), dt))

        X = sb("X", [128, KC, KVT], F32)
        Kt = sb("Kt", [128, 4, KVT], BF16)
        Vt = sb("Vt", [128, NPAIR, 512], BF16)
        Kc = sb("Kc", [128, 4, CTXT], BF16)
        Vc = sb("Vc", [128, 2, 512], BF16)
        wring = [sb("wring%d" % i, [128, 4096], BF16) for i in range(2)]
        XNW = 896
        xn = sb("xn", [128, KC, XNW], BF16)
        tdiv = sb("tdiv", [128, 2, 512], F32)
        TBall = sb("TBall", [128, 2112], F32)
        TBv = TBall[:, :].bitcast(BF16)
        TBb = [TBv[:, i * 1408:i * 1408 + 1152] for i in range(3)]
        TBm = [TBv[:, i * 1408 + 1152:i * 1408 + 1408] for i in range(3)]
        TBs = TBb
        identb = sb("identb", [128, 128], BF16)
        PWB = sb("PWB", [128, 4, 128], BF16)
        wring.append(TBv[:, 0:4096])
        vecs = sb("vecs", [128, NV], F32)
        ccs = sb("ccs", [128, 16], F32)
        scc = sb("scc", [128, 16], BF16)
        Msb = [sb("Msb%d" % l, [128, 72, 2], F32) for l in range(2)]
        AV = sb("AV", [128, 2 * 3 * 2 * 8], F32)
        GV = sb("GV", [128, 2 * 3 * 2 * 8], F32)
        ones = sb("ones", [128, 128], BF16)
        YH = sb("YH", [128, KC, 8], F32)
        ARENA_B = 27904
        arena = sb("arena", [128, ARENA_B // 2], BF16)
        psb = [es.enter_context(nc.psum_tensor("ps%d" % i, [128, 512], F32)) for i in range(8)]

        class St:
            xn = None
            G = None
            cap = 896
            pre = None
            nw = 3
            nbanks = 7
            bank = 0
            wslot = 0
            sa_i = 0
            td_i = 0

        def aview(boff, nelem, dt, base=None, cap=None):
            assert boff % 4 == 0
            if base is None:
                base, cap = arena, ARENA_B
            if dt == BF16:
                assert boff + 2 * nelem <= cap, (boff, nelem)
                return base[:, boff // 2: boff // 2 + nelem]
            assert boff + 4 * nelem <= cap, (boff, nelem)
            return base[:, boff // 2: boff // 2 + 2 * nelem].bitcast(F32)

        def newbank():
            i = St.bank
            St.bank = (i + 1) % St.nbanks
            return psb[i], ("ps", i)

        def xblocks(t0, n):
            return [("X", b) for b in range(t0 // 64, (t0 + n - 1) // 64 + 1)]

        def vcol(name, idx=0, n=1):
            o = LAY[name] + idx
            return vecs[:, o:o + n]

        def avi(l, s, col, fc):
            return ((l * 3 + s) * 2 + col) * 8 + fc

        def Acol(l, s, col, fc):
            i = avi(l, s, col, fc)
            return AV[:, i:i + 1]

        def Gcol(l, s, col, fc):
            i = avi(l, s, col, fc)
            return GV[:, i:i + 1]

        def Bcol(l, s, col, fc):
            return Msb[l][:, (3 * s) * 8 + fc, col:col + 1]

        def mmgroup(out_ap, lhs_list, rhs_list, reads, pskey):
            n = len(lhs_list)
            for i in range(n):
                fw.op("pe", lambda e, o=out_ap, a=lhs_list[i], b=rhs_list[i], st=(i == 0), sp=(i == n - 1):
                      e.matmul(o, lhsT=a, rhs=b, start=st, stop=sp),
                      reads=reads, writes=[pskey], inc=(i == n - 1), skip_self=True)

        def wload(parts):
            slot = St.wslot
            St.wslot = (slot + 1) % St.nw
            off = 0
            views = []
            pairs = []
            for ap in parts:
                K, ncols = ap.shape
                kc = K // 128
                assert off + kc * ncols <= 4096
                view = wring[slot][:, off: off + kc * ncols].rearrange("p (k m) -> p k m", k=kc)
                pairs.append((view, ap.rearrange("(k p) m -> p k m", p=128)))
                views.append(view)
                off += kc * ncols
            key = ("w", slot)
            fw.dma("pool", "w%d" % slot, pairs,
                   writes=[key] + ([("tb", i) for i in range(3)] + [("tbm", i) for i in range(3)] if slot == 2 else []))
            return views, key

        def act(out, in_, func, reads, writes, **kw):
            fw.op("act", lambda e: e.activation(out=out, in_=in_, func=func, **kw), reads=reads, writes=writes)

        def dve_tt(out, in0, in1, op, reads, writes):
            fw.op("dve", lambda e: e.tensor_tensor(out=out, in0=in0, in1=in1, op=op), reads=reads, writes=writes)

        def dve_stt(out, in0, scalar, in1, op0, op1, reads, writes):
            fw.op("dve", lambda e: e.scalar_tensor_tensor(out=out, in0=in0, scalar=scalar, in1=in1, op0=op0, op1=op1),
                  reads=reads, writes=writes)

        def dve_ts(out, in0, s1, s2, op0, op1, reads, writes):
            if s2 is None:
                fw.op("dve", lambda e: e.tensor_scalar(out=out, in0=in0, scalar1=s1, scalar2=None, op0=op0),
                      reads=reads, writes=writes)
            else:
                fw.op("dve", lambda e: e.tensor_scalar(out=out, in0=in0, scalar1=s1, scalar2=s2, op0=op0, op1=op1),
                      reads=reads, writes=writes)

        def dve_copy(out, in_, reads, writes):
            fw.op("dve", lambda e: e.tensor_copy(out=out, in_=in_), reads=reads, writes=writes)

        def resid_update(bank, bk, oc, s, N, gcol, defer, gkey):
            main = N - defer
            xk = xblocks(s, main)
            dve_stt(X[:, oc, s:s + main], bank[:, 0:main], gcol, X[:, oc, s:s + main], ALU.mult, ALU.add,
                    reads=[bk, gkey] + xk, writes=xk)
            if defer:
                dve_ts(YH[:, oc, 0:defer], bank[:, main:N], gcol, None, ALU.mult, None, reads=[bk, gkey], writes=["YH"])

        def apply_deferred(end, defer):
            xk = xblocks(end - defer, defer)
            dve_tt(X[:, :, end - defer:end], X[:, :, end - defer:end], YH[:, :, 0:defer], ALU.add, reads=["YH"] + xk, writes=xk)

        SQ_OFF = 19712

        class NS:
            sqoff = SQ_OFF
            sqkeys = ["sq"]
            rinv = None
            base = None
            cap = None

        def norm_stats(x3, n, xkeys):
            sk = NS.sqkeys
            sqv = aview(NS.sqoff, 8 * n, BF16, NS.base, NS.cap).rearrange("p (c n) -> p c n", c=8)
            sd = aview(NS.sqoff, n, F32, NS.base, NS.cap)
            rinv = aview(NS.sqoff + 4 * n, n, F32, NS.base, NS.cap)
            NS.rinv = rinv
            act(sqv, x3, AF.Square, reads=xkeys, writes=sk)
            bank, bk = newbank()
            mmgroup(bank[:, 0:n], [ones[:, :]] * KC, [sqv[:, c, :] for c in range(KC)], reads=sk + ["ones"], pskey=bk)
            act(sd, bank[:, 0:n], AF.Sqrt, reads=[bk], writes=sk, scale=1.0 / D, bias=1e-6)
            fw.op("dve", lambda e: e.reciprocal(out=rinv, in_=sd), reads=sk, writes=sk)

        def norm_mod(x3, n, xkeys, l, s, col, dst3, dkeys):
            norm_stats(x3, n, xkeys)
            for c in range(KC):
                i = St.td_i
                St.td_i ^= 1
                dve_tt(tdiv[:, i, 0:n], x3[:, c, :], NS.rinv, ALU.mult, reads=xkeys + NS.sqkeys, writes=[("td", i)])
                act(dst3[:, c, :], tdiv[:, i, 0:n], AF.Identity, reads=[("td", i), ("AV", l, s), ("M", l, s)], writes=dkeys,
                    scale=Acol(l, s, col, c), bias=Bcol(l, s, col, c))

        def norm_sq(x3, n, xkeys):
            sqv = aview(NS.sqoff, 8 * n, BF16, NS.base, NS.cap).rearrange("p (c n) -> p c n", c=8)
            act(sqv, x3, AF.Square, reads=xkeys, writes=NS.sqkeys)

        def norm_fin(x3, n, xkeys, l, s, col, dst3, dkeys):
            sk = NS.sqkeys
            sqv = aview(NS.sqoff, 8 * n, BF16, NS.base, NS.cap).rearrange("p (c n) -> p c n", c=8)
            sd = aview(NS.sqoff, n, F32, NS.base, NS.cap)
            rinv = aview(NS.sqoff + 4 * n, n, F32, NS.base, NS.cap)
            bank, bk = newbank()
            mmgroup(bank[:, 0:n], [ones[:, :]] * KC, [sqv[:, c, :] for c in range(KC)], reads=sk + ["ones"], pskey=bk)
            act(sd, bank[:, 0:n], AF.Sqrt, reads=[bk], writes=sk, scale=1.0 / D, bias=1e-6)
            fw.op("dve", lambda e: e.reciprocal(out=rinv, in_=sd), reads=sk, writes=sk)
            for c in range(KC):
                i = St.td_i
                St.td_i ^= 1
                dve_tt(tdiv[:, i, 0:n], x3[:, c, :], rinv, ALU.mult, reads=xkeys + sk, writes=[("td", i)])
                act(dst3[:, c, :], tdiv[:, i, 0:n], AF.Identity, reads=[("td", i), ("AV", l, s), ("M", l, s)], writes=dkeys,
                    scale=Acol(l, s, col, c), bias=Bcol(l, s, col, c))

        G_ = aview(0, 11 * 896, BF16).rearrange("p (c n) -> p c n", c=11)

        St.xn = xn
        St.G = G_

        class Hook:
            side = []
            every = 1
            cnt = 0

        def run_side(force_all=False):
            if force_all:
                while Hook.side:
                    Hook.side.pop(0)()
                return
            Hook.cnt += 1
            if Hook.side and Hook.cnt % Hook.every == 0:
                Hook.side.pop(0)()

        def tbuf(kind):
            return X if kind == "x" else Xctx

        def tkeys(kind, t0, n):
            return xblocks(t0, n) if kind == "x" else ["Xctx"]

        def ffn(l, f, tiles, nxt=None):
            s = 0 if f == 0 else 2
            offs = []
            o = 0
            for (kind, t0, n) in tiles:
                offs.append(o)
                o += n
            assert o <= St.cap
            xn = St.xn
            G_ = St.G
            if St.pre is not None:
                assert St.pre == (l, f, tuple(tiles)), (St.pre, l, f, tiles)
                St.pre = None
            else:
                for ti, (kind, t0, n) in enumerate(tiles):
                    norm_mod(tbuf(kind)[:, :, t0:t0 + n], n, tkeys(kind, t0, n), l, s, 0 if kind == "x" else 1,
                             xn[:, :, offs[ti]:offs[ti] + n], [("xn", ti)])
            stages = []
            if nxt is not None:
                l2, f2, tiles2 = nxt
                assert [t[2] for t in tiles2] == [t[2] for t in tiles]
                s2 = 0 if f2 == 0 else 2
                nargs = [(tbuf(kind)[:, :, t0:t0 + n], n, tkeys(kind, t0, n), l2, s2, 0 if kind == "x" else 1,
                          xn[:, :, offs[ti]:offs[ti] + n], [("xn", ti)]) for ti, (kind, t0, n) in enumerate(tiles2)]

                def stage(k):
                    if k >= 1:
                        norm_fin(*nargs[k - 1])
                    if k < len(nargs):
                        norm_sq(*nargs[k][:3])
                stages = [lambda k=k: stage(k) for k in range(len(nargs) + 1)]
                assert len(stages) <= 5
                St.pre = (l2, f2, tuple(tiles2))
            wA = w13[l][f]
            wB = w2[l][f]
            for half in range(2):
                for cp in range(0, 11, 2):
                    cls = [c for c in (cp, cp + 1) if c < 11]
                    ncl = len(cls)
                    c0 = half * 11 + cp
                    (wa, wb), wk = wload([wA[:, c0 * 128:(c0 + ncl) * 128], wA[:, DFF + c0 * 128: DFF + (c0 + ncl) * 128]])
                    for j, cl in enumerate(cls):
                        for ti, (kind, t0, n) in enumerate(tiles):
                            xo = offs[ti]
                            pa, pak = newbank()
                            pb, pbk = newbank()
                            rhs = [xn[:, k, xo:xo + n] for k in range(KC)]
                            mmgroup(pa[:, 0:n], [wa[:, k, j * 128:(j + 1) * 128] for k in range(KC)], rhs, reads=[wk, ("xn", ti)], pskey=pak)
                            mmgroup(pb[:, 0:n], [wb[:, k, j * 128:(j + 1) * 128] for k in range(KC)], rhs, reads=[wk, ("xn", ti)], pskey=pbk)
                            i = St.td_i
                            St.td_i ^= 1
                            act(tdiv[:, i, 0:n], pa[:, 0:n], AF.Silu, reads=[pak], writes=[("td", i)])
                            dve_tt(G_[:, cl, xo:xo + n], pb[:, 0:n], tdiv[:, i, 0:n], ALU.mult, reads=[pbk, ("td", i)], writes=[("g", cl, ti)])
                    run_side()
                if half == 1 and stages:
                    stages.pop(0)()
                for oc0 in range(0, 8, 2):
                    (wv,), wk = wload([wB[half * 1408:(half + 1) * 1408, oc0 * 128:(oc0 + 2) * 128]])
                    for j in range(2):
                        oc = oc0 + j
                        for ti, (kind, t0, n) in enumerate(tiles):
                            xo = offs[ti]
                            col = 0 if kind == "x" else 1
                            xbuf = tbuf(kind)
                            py, pyk = newbank()
                            mmgroup(py[:, 0:n], [wv[:, k, j * 128:(j + 1) * 128] for k in range(11)],
                                    [G_[:, k, xo:xo + n] for k in range(11)],
                                    reads=[wk] + [("g", k, ti) for k in range(11)], pskey=pyk)
                            xk = tkeys(kind, t0, n)
                            dve_stt(xbuf[:, oc, t0:t0 + n], py[:, 0:n], Gcol(l, s, col, oc), xbuf[:, oc, t0:t0 + n], ALU.mult, ALU.add,
                                    reads=[pyk, ("GV", l, s)] + xk, writes=xk)
                    run_side()
                    if half == 1 and stages:
                        stages.pop(0)()
            assert not stages

        def kv_project(tiles):
            offs = []
            o = 0
            for (kind, t0, n) in tiles:
                offs.append(o)
                o += n
            for ti, (kind, t0, n) in enumerate(tiles):
                norm_mod(tbuf(kind)[:, :, t0:t0 + n], n, tkeys(kind, t0, n), 0, 1, 0 if kind == "x" else 1,
                         xn[:, :, offs[ti]:offs[ti] + n], [("xn", ti)])
            (wv,), wk = wload([w_in[:, 512:1024]])
            for oc in range(4):
                for ti, (kind, t0, n) in enumerate(tiles):
                    xo = offs[ti]
                    bank, bk = newbank()
                    mmgroup(bank[:, 0:n], [wv[:, k, oc * 128:(oc + 1) * 128] for k in range(KC)], [xn[:, k, xo:xo + n] for k in range(KC)],
                            reads=[wk, ("xn", ti)], pskey=bk)
                    if kind == "x":
                        act(Kt[:, oc, t0:t0 + n], bank[:, 0:n], AF.Copy, reads=[bk],
                            writes=[("K", b) for b in range(t0 // 128, (t0 + n) // 128)] + ["Xctx"])
                    else:
                        act(Kc[:, oc, t0:t0 + n], bank[:, 0:n], AF.Copy, reads=[bk], writes=["Kc"])
            (wv,), wk = wload([w_in[:, 1024:1536]])
            for ti, (kind, t0, n) in enumerate(tiles):
                xo = offs[ti]
                for b in range(n // 128):
                    bank, bk = newbank()
                    mmgroup(bank[:, 0:512], [xn[:, k, xo + b * 128: xo + (b + 1) * 128] for k in range(KC)], [wv[:, k, 0:512] for k in range(KC)],
                            reads=[wk, ("xn", ti)], pskey=bk)
                    pr = t0 // 128 + b
                    if kind == "x":
                        dve_copy(Vt[:, pr, :], bank[:, 0:512], reads=[bk], writes=[("V", pr)])
                    else:
                        dve_copy(Vc[:, pr, :], bank[:, 0:512], reads=[bk], writes=["Vc"])

        def finish_debug():
            allx = [("X", b) for b in range(KVT // 64)]
            fw.dma("sp", "dbg", [(xdbg.rearrange("(c p) t -> p c t", p=128), X[:, :, :])], reads=allx)
            fw.dma("sp", "dbg", [(mdbg[:, 0:144], Msb[0][:, :, :].rearrange("p j t -> p (j t)")),
                                 (mdbg[:, 144:288], Msb[1][:, :, :].rearrange("p j t -> p (j t)"))],
                   reads=[("M", l, s_) for l in range(2) for s_ in range(3)])
            fw.dma("sp", "dbg", [(kdbg[:, :], Kt[:, :, :].rearrange("p c n -> p (c n)")),
                                 (vdbg[:, :], Vt[:, :, :].rearrange("p c n -> p (c n)"))],
                   reads=[("K", b) for b in range(NPAIR)] + [("V", b) for b in range(NPAIR)])
            fw.emit()

        xTv = xT.rearrange("(c p) t -> p c t", p=128)
        fw.dma("sp", "ld0", [(vecs[:, :], vecs_d[:, :]), (ccs[:, :], cc_d[:, :])], writes=["vecs", "ccs"])
        fw.op("dve", lambda e: e.memset(ones[:, :], 1.0), writes=["ones"])
        for c in range(KC):
            fw.dma("sp" if c % 2 == 0 else "act", "ldx%d" % c, [(X[:, c, :], xTv[:, c, :])], writes=[("Xc_", c)])
        fw.op("dve", lambda e: e.memset(tdiv[:, 0, 0:1], 0.0), reads=[("Xc_", c) for c in range(KC)], writes=[("X", b) for b in range(KVT // 64)] + [("td", 0)])
        act(scc[:, :], ccs[:, :], AF.Silu, reads=["ccs"], writes=["scc"])
        fw.dma("sp", "ld0", [(tdiv[:, 1, 0:128], ident_d[:, :])], writes=[("td", 1)])
        act(identb[:, :], tdiv[:, 1, 0:128], AF.Copy, reads=[("td", 1)], writes=["identb"])
        scc3 = scc[:, :].rearrange("p (k t) -> p k t", t=2)

        Xctx = Kt[:, :, :].rearrange("p c n -> p (c n)")[:, 0:2 * KC * CTXT].bitcast(F32).rearrange("p (c n) -> p c n", c=KC)
        cT = ctxT.rearrange("(c p) t -> p c t", p=128)
        fw.dma("sp", "ldc", [(Xctx[:, :, :], cT)], writes=["Xctx"])

        mbank = psb[7]
        mvs = [mbank[:, l * 144:(l + 1) * 144].rearrange("p (j t) -> p j t", t=2) for l in range(2)]

        def adaln_group(l, grp):
            mv = mvs[l]
            (wv,), wk = wload([modw[l][:, grp * 512:(grp + 1) * 512]])
            for j in range(4):
                jf = grp * 4 + j
                for k in range(KC):
                    fw.op("pe", lambda e, o=mv[:, jf, :], a=wv[:, k, j * 128:(j + 1) * 128], b=scc3[:, k, :], st=(k == 0), sp=(k == KC - 1):
                          e.matmul(o, lhsT=a, rhs=b, start=st, stop=sp),
                          reads=[wk, "scc"], writes=[("mps", l, jf // 24)], inc=(k == KC - 1), skip_self=True)

        def adaln_final(l, s_):
            mv = mvs[l]
            mb = vcol("modb", l * 72 + 24 * s_, 24)
            dve_tt(Msb[l][:, 24 * s_:24 * s_ + 24, :], mv[:, 24 * s_:24 * s_ + 24, :], mb.unsqueeze(2).to_broadcast([128, 24, 2]), ALU.add,
                   reads=[("mps", l, s_), "vecs"], writes=[("M", l, s_)])
            for col in range(2):
                i0 = avi(l, s_, col, 0)
                ngv = vcol("ng", (l * 3 + s_) * 8, 8)
                dve_stt(AV[:, i0:i0 + 8], Msb[l][:, (3 * s_ + 1) * 8:(3 * s_ + 2) * 8, col], 1.0, ngv, ALU.add, ALU.mult,
                        reads=[("M", l, s_), "vecs"], writes=[("AV", l, s_)])
                dve_ts(GV[:, i0:i0 + 8], Msb[l][:, (3 * s_ + 2) * 8:(3 * s_ + 3) * 8, col], 0.5 if s_ != 1 else 1.0, None, ALU.mult, None,
                       reads=[("M", l, s_)], writes=[("GV", l, s_)])

        for grp in range(6):
            adaln_group(0, grp)
        adaln_final(0, 0)
        for s_ in (1, 2):
            for grp in range(6 * s_, 6 * s_ + 6):
                Hook.side.append(lambda grp=grp: adaln_group(0, grp))
            Hook.side.append(lambda s_=s_: adaln_final(0, s_))
        side_l1 = []
        for s_ in range(3):
            for grp in range(6 * s_, 6 * s_ + 6):
                side_l1.append(lambda grp=grp: adaln_group(1, grp))
            side_l1.append(lambda s_=s_: adaln_final(1, s_))

        p1_super = [[("x", 0, 512), ("c", 0, 256)],
                    [("x", 512, 512), ("x", 1024, 256)],
                    [("x", 1280, 512), ("x", 1792, 256)],
                    [("x", 2048, 512), ("x", 2560, 128)]]
        for sti, tiles in enumerate(p1_super):
            if sti == 1:
                Hook.side.extend(side_l1)
                Hook.every = 3
            ffn(0, 0, tiles)
            if sti == 0:
                run_side(force_all=True)
            kv_project(tiles)
        run_side(force_all=True)
        Hook.every = 1
        St.nbanks = 8

        fw.barrier()
        if stop_after == 1:
            finish_debug()
            return nc
        QT = aview(0, 4 * 256, BF16).rearrange("p (c n) -> p c n", c=4)
        U = aview(2048, 4 * 272, F32).rearrange("p (c n) -> p c n", c=4)
        PT_ = [aview(6400 + i * 1088, 272, F32) for i in range(2)]
        DD = aview(8576, 4 * 256, BF16).rearrange("p (c n) -> p c n", c=4)
        MIX = aview(10624, 8 * 256, BF16).rearrange("p (c n) -> p c n", c=8)
        MSP = [aview(14720 + i * 1792, 7 * 128, BF16).rearrange("p (s q) -> p s q", q=128) for i in range(2)]
        PTb = [aview(21888 + i * 2304, 9 * 128, BF16) for i in range(2)]
        RD = aview(26496, 2 * 128, F32).rearrange("p (i n) -> p i n", i=2)
        T8 = aview(27520, 8, F32)
        cmask = vcol("cmask", 0, 64)
        fw.dma("pool", "pwb", [(PWB[:, :, :], poolw.rearrange("(k p) m -> p k m", p=128))], writes=["PWB"])

        def mixer0_tile(e0, N):
            s = EOFF + e0
            n2 = N + 16
            norm_mod(X[:, :, s - 8:s + N + 8], n2, xblocks(s - 8, n2), 0, 1, 0, xn[:, :, 0:n2], [("xn", 0)])
            if e0 > 0:
                apply_deferred(s, 8)
            (wv,), wk = wload([w_in[:, 1536:2048]])
            for gi in range(4):
                bank, bk = newbank()
                mmgroup(bank[:, 0:n2], [wv[:, k, gi * 128:(gi + 1) * 128] for k in range(KC)], [xn[:, k, 0:n2] for k in range(KC)],
                        reads=[wk, ("xn", 0)], pskey=bk)
                act(U[:, gi, 0:n2], bank[:, 0:n2], AF.Copy, reads=[bk], writes=["U"])
            (wv,), wk = wload([w_in[:, 0:512]])
            for oc in range(4):
                bank, bk = newbank()
                mmgroup(bank[:, 0:N], [wv[:, k, oc * 128:(oc + 1) * 128] for k in range(KC)], [xn[:, k, 8:8 + N] for k in range(KC)],
                        reads=[wk, ("xn", 0)], pskey=bk)
                act(QT[:, oc, 0:N], bank[:, 0:N], AF.Copy, reads=[bk], writes=["QT"], scale=0.125)
            wo = [wload([w_out[:, grp * 512:(grp + 1) * 512]]) for grp in range(2)]
            for which, (lo, hi) in enumerate(((56, 64), (2112, 2120))):
                a = max(lo, e0 - 8)
                b = min(hi, e0 + N + 8)
                if a < b:
                    ca = a - (e0 - 8)
                    cb = b - (e0 - 8)
                    dve_ts(U[:, :, ca:cb], U[:, :, ca:cb], vcol("vm", which), None, ALU.mult, None, reads=["U", "vecs"], writes=["U"])
            for gi, w in enumerate(POOL_W):
                cur = U[:, gi, 0:n2]
                width = n2
                step = 1
                pi = 0
                rk = ["U"]
                while step < w:
                    nw_ = width - step
                    dst = PT_[pi][:, 0:nw_]
                    dve_tt(dst, cur[:, 0:nw_], cur[:, step:step + nw_], ALU.add, reads=rk, writes=[("pt", pi)])
                    rk = [("pt", pi)]
                    cur = PT_[pi][:, 0:nw_]
                    pi ^= 1
                    width = nw_
                    step *= 2
                c0 = 8 - w // 2
                dve_stt(DD[:, gi, 0:N], cur[:, c0:c0 + N], 1.0 / w, U[:, gi, 8:8 + N], ALU.mult, ALU.subtract,
                        reads=rk + ["U"], writes=["DD"])
                for which, lo in enumerate((64, 2104)):
                    if e0 <= lo and lo + 8 <= e0 + N:
                        i0 = lo - e0
                        ic = vcol("icnt", (which * 4 + gi) * 8, 8)
                        dve_tt(T8[:, 0:8], cur[:, c0 + i0:c0 + i0 + 8], ic, ALU.mult, reads=rk + ["vecs"], writes=["T8"])
                        dve_tt(DD[:, gi, i0:i0 + 8], T8[:, 0:8], U[:, gi, 8 + i0:8 + i0 + 8], ALU.subtract, reads=["T8", "U"], writes=["DD"])
            units = []
            for h in range(8):
                for m in range(e0 // 128, (e0 + N) // 128):
                    units.append((h, m))

            def emit_tb_dma(h):
                slot = h % 3
                fw.dma("pool", "tb%d" % slot, [(TBb[slot], tb_d[h])], writes=[("tb", slot), ("tbm", slot), ("w", 2)])

            def emit_tb_prep(h):
                slot = h % 3
                t3 = TBb[slot].rearrange("p (j q) -> p j q", q=64)
                dve_tt(t3, t3, cmask.unsqueeze(1).to_broadcast([128, 18, 64]), ALU.add, reads=[("tb", slot), "vecs"], writes=[("tb", slot)])
                for idx, (jj, mi) in enumerate(((4, 0), (5, 1), (12, 2), (13, 3))):
                    dve_ts(TBm[slot][:, idx * 64:(idx + 1) * 64], TBb[slot][:, jj * 64:(jj + 1) * 64], vcol("mreg", mi), None, ALU.add, None,
                           reads=[("tb", slot), "vecs"], writes=[("tbm", slot)])

            spec_in_tile = [m for m in range(e0 // 128, (e0 + N) // 128) if m in SPEC_UNITS]
            for idx, m in enumerate(spec_in_tile):
                desc = unit_pairs_desc(m)
                for pi_ in range(len(desc)):
                    for a in range(2):
                        act(MSP[idx][:, pi_, a * 64:(a + 1) * 64], cmask, AF.Identity, reads=["vecs"], writes=[("msp", idx)],
                            scale=0.0, bias=vcol(("mspec", m), pi_ * 2 + a))

            def emit_qk(ui):
                h, m = units[ui]
                desc = unit_pairs_desc(m)
                special = m in SPEC_UNITS
                hp = (h % 2) * 64
                hc = h // 2
                qc0 = m * 128 - e0
                q = QT[hp:hp + 64, hc, qc0:qc0 + 128]
                banks = [newbank(), newbank()]
                if special or not CTX_SHARE:
                    cbank, cbk = newbank()
                    ccol = 0
                else:
                    cbank, cbk = banks[1]
                    ccol = 128
                slot = h % 3
                for pi_, p in enumerate(desc):
                    bank, bk = banks[pi_ // 4]
                    col = (pi_ % 4) * 128
                    fw.op("pe", lambda e, o=bank[:, col:col + 128], a=Kt[hp:hp + 64, hc, p * 128:(p + 1) * 128], b=q, st=(pi_ % 4 == 0):
                          e.matmul(o, lhsT=a, rhs=b, start=st, stop=False, skip_group_check=True),
                          reads=[("K", p), "QT"], writes=[bk], inc=False, skip_self=True)
                for pi_, p in enumerate(desc):
                    bank, bk = banks[pi_ // 4]
                    col = (pi_ % 4) * 128
                    if special:
                        jj0 = 12 - 2 * (p - m)
                        assert 0 <= jj0 <= 16
                        bt = TBb[slot][:, jj0 * 64:(jj0 + 2) * 64]
                    elif pi_ == 0:
                        bt = TBm[slot][:, 0:128]
                    elif pi_ == 4:
                        bt = TBm[slot][:, 128:256]
                    else:
                        bt = TBb[slot][:, (4 + 2 * pi_) * 64:(6 + 2 * pi_) * 64]
                    fw.op("pe", lambda e, o=bank[:, col:col + 128], b=bt:
                          e.matmul(o, lhsT=identb[:, :], rhs=b, start=False, stop=True, skip_group_check=True),
                          reads=[("tb", slot), ("tbm", slot), "identb"], writes=[bk], inc=False, skip_self=True)
                    if special:
                        sidx = spec_in_tile.index(m)
                        fw.op("pe", lambda e, o=bank[:, col:col + 128], b=MSP[sidx][:, pi_, :]:
                              e.matmul(o, lhsT=identb[:, :], rhs=b, start=False, stop=True, skip_group_check=True),
                              reads=[("msp", sidx), "identb"], writes=[bk], inc=False, skip_self=True)
                for b_ in range(2):
                    fw.op("pe", lambda e, o=cbank[:, ccol + b_ * 128:ccol + (b_ + 1) * 128], a=Kc[hp:hp + 64, hc, b_ * 128:(b_ + 1) * 128], b=q:
                          e.matmul(o, lhsT=a, rhs=b, start=True, stop=True),
                          reads=["Kc", "QT"], writes=[cbk], inc=(b_ == 1), skip_self=True)
                return dict(h=h, m=m, desc=desc, banks=banks, cbank=(cbank, cbk), ccol=ccol, ui=ui)

            def emit_soft(stt):
                h, m, desc, ui = stt["h"], stt["m"], stt["desc"], stt["ui"]
                npos = len(desc)
                P = PTb[ui % 2]
                pk = ("P", ui % 2)
                (ba, bak), (bb, bbk) = stt["banks"]
                act(P[:, 0:512], ba[:, 0:512], AF.Exp, reads=[bak], writes=[pk])
                act(P[:, 512:npos * 128], bb[:, 0:(npos - 4) * 128], AF.Exp, reads=[bbk], writes=[pk])
                cbank, cbk = stt["cbank"]
                cc_ = stt["ccol"]
                act(P[:, npos * 128:(npos + 2) * 128], cbank[:, cc_:cc_ + 256], AF.Exp, reads=[cbk], writes=[pk])

            def emit_pv(stt):
                h, m, desc, ui = stt["h"], stt["m"], stt["desc"], stt["ui"]
                hp = (h % 2) * 64
                hc = h // 2
                qc0 = m * 128 - e0
                npos = len(desc)
                P = PTb[ui % 2]
                pk = ("P", ui % 2)
                ob, obk = newbank()
                lhs = [Vt[:, p, h * 64:(h + 1) * 64] for p in desc] + [Vc[:, b_, h * 64:(h + 1) * 64] for b_ in range(2)]
                rhs = [P[:, i * 128:(i + 1) * 128] for i in range(npos + 2)]
                rd = [("V", p) for p in desc] + ["Vc", pk]
                mmgroup(ob[hp:hp + 64, 0:128], lhs, rhs, reads=rd, pskey=obk)
                mmgroup(ob[hp:hp + 64, 128:256], [ones[:, 0:64]] * (npos + 2), rhs, reads=[pk, "ones"], pskey=obk)
                ri = ui % 2
                fw.op("dve", lambda e: e.reciprocal(out=RD[hp:hp + 64, ri, :], in_=ob[hp:hp + 64, 128:256]), reads=[obk], writes=[("rd", ri)])
                dve_tt(MIX[hp:hp + 64, hc, qc0:qc0 + 128], ob[hp:hp + 64, 0:128], RD[hp:hp + 64, ri, :], ALU.mult,
                       reads=[obk, ("rd", ri)], writes=[("mix", hc)])

            upr = len(units) // 8
            infl = []
            emit_tb_dma(0)
            emit_tb_dma(1)
            emit_tb_prep(0)
            for ui in range(len(units)):
                h = units[ui][0]
                first = (ui % upr == 0)
                infl.append(emit_qk(ui))
                if first and h + 2 < 8:
                    emit_tb_dma(h + 2)
                if ui >= 1:
                    emit_soft(infl[ui - 1])
                    emit_pv(infl[ui - 1])
                if first and h + 1 < 8:
                    emit_tb_prep(h + 1)
            emit_soft(infl[-1])
            emit_pv(infl[-1])
            for gi in range(4):
                bank, bk = newbank()
                mmgroup(bank[:, 0:N], [PWB[:, gi, :]], [DD[:, gi, 0:N]], reads=["PWB", "DD"], pskey=bk)
                act(MIX[:, 4 + gi, 0:N], bank[:, 0:N], AF.Identity, reads=[bk, "vecs"], writes=[("mix", 4 + gi)],
                    scale=vcol("pscale", gi), bias=0.0)
            for grp in range(2):
                (wv,), wk = wo[grp]
                for j in range(4):
                    oc = grp * 4 + j
                    bank, bk = newbank()
                    mmgroup(bank[:, 0:N], [wv[:, k, j * 128:(j + 1) * 128] for k in range(KC)], [MIX[:, k, 0:N] for k in range(KC)],
                            reads=[wk] + [("mix", k) for k in range(KC)], pskey=bk)
                    resid_update(bank, bk, oc, s, N, Gcol(0, 1, 0, oc), 8, ("GV", 0, 1))
            if e0 + N >= ET:
                apply_deferred(s + N, 8)

        St.nw = 2
        St.wslot = St.wslot % 2
        NS.sqoff = 14720
        NS.sqkeys = [("msp", 0), ("msp", 1), "sq2"]
        for (e0, N) in split_tiles(0, ET, tmax=256):
            mixer0_tile(e0, N)
        NS.sqoff = SQ_OFF
        NS.sqkeys = ["sq"]
        St.nw = 3

        fw.barrier()
        if stop_after == 2:
            finish_debug()
            return nc
        o1 = OWNOFF - 1
        BIGCAP = 1026
        St.xn = Kt[:, :, :].rearrange("p c n -> p (c n)")[:, 0:8 * BIGCAP].rearrange("p (c n) -> p c n", c=8)
        St.G = aview(0, 11 * BIGCAP, BF16).rearrange("p (c n) -> p c n", c=11)
        St.cap = BIGCAP
        NS.base = Vt[:, :, :].rearrange("p c n -> p (c n)")
        NS.cap = 2 * NPAIR * 512
        NS.sqoff = 0
        e_super = [[("x", o1, 512), ("x", o1 + 512, 342), ("x", o1 + 854, 171)],
                   [("x", o1 + 1025, 512), ("x", o1 + 1537, 342), ("x", o1 + 1879, 171)]]
        ffn(0, 1, e_super[0], nxt=(0, 1, e_super[1]))
        ffn(0, 1, e_super[1], nxt=(1, 0, e_super[0]) if stop_after is None else None)
        if stop_after == 3:
            finish_debug()
            return nc
        ffn(1, 0, e_super[0], nxt=(1, 0, e_super[1]))
        ffn(1, 0, e_super[1])

        fw.barrier()
        if stop_after == 4:
            finish_debug()
            return nc
        CG = aview(0, 2 * 456, F32).rearrange("p (i n) -> p i n", i=2)
        ZZ = aview(3648, 2 * 456, F32).rearrange("p (i n) -> p i n", i=2)
        YY = aview(7296, 2 * 456, F32).rearrange("p (i n) -> p i n", i=2)
        GZ = aview(10944, 8 * 448, BF16).rearrange("p (c n) -> p c n", c=8)

        own_tiles = split_tiles(OWNOFF, OWNT, tmax=448)
        for tix, (s, N) in enumerate(own_tiles):
            n2 = N + 2
            norm_mod(X[:, :, s - 1:s + N + 1], n2, xblocks(s - 1, n2), 1, 1, 0, xn[:, :, 0:n2], [("xn", 0)])
            if tix > 0:
                apply_deferred(s, 1)
            for c in range(KC):
                (wbg, wcg, wxi), wk = wload([cwin[:, c * 128:(c + 1) * 128], cwin[:, D + c * 128: D + (c + 1) * 128],
                                             cwin[:, 2 * D + c * 128: 2 * D + (c + 1) * 128]])
                pc, pck = newbank()
                px, pxk = newbank()
                pg, pgk = newbank()
                rhs2 = [xn[:, k, 0:n2] for k in range(KC)]
                mmgroup(pc[:, 0:n2], [wcg[:, k, :] for k in range(KC)], rhs2, reads=[wk, ("xn", 0)], pskey=pck)
                mmgroup(px[:, 0:n2], [wxi[:, k, :] for k in range(KC)], rhs2, reads=[wk, ("xn", 0)], pskey=pxk)
                mmgroup(pg[:, 0:N], [wbg[:, k, :] for k in range(KC)], [xn[:, k, 1:1 + N] for k in range(KC)], reads=[wk, ("xn", 0)], pskey=pgk)
                i = c % 2
                act(CG[:, i, 0:n2], pc[:, 0:n2], AF.Copy, reads=[pck], writes=[("cg", i)])
                dve_tt(ZZ[:, i, 0:n2], px[:, 0:n2], CG[:, i, 0:n2], ALU.mult, reads=[pxk, ("cg", i)], writes=[("zz", i)])
                if tix == 0:
                    dve_ts(ZZ[:, i, 0:1], ZZ[:, i, 0:1], vcol("vm", 0), None, ALU.mult, None, reads=[("zz", i), "vecs"], writes=[("zz", i)])
                if tix == len(own_tiles) - 1:
                    dve_ts(ZZ[:, i, n2 - 1:n2], ZZ[:, i, n2 - 1:n2], vcol("vm", 1), None, ALU.mult, None, reads=[("zz", i), "vecs"], writes=[("zz", i)])
                dve_ts(YY[:, i, 0:N], ZZ[:, i, 0:N], vcol("convw", 0 * 8 + c), None, ALU.mult, None, reads=[("zz", i), "vecs"], writes=[("yy", i)])
                dve_stt(YY[:, i, 0:N], ZZ[:, i, 1:1 + N], vcol("convw", 1 * 8 + c), YY[:, i, 0:N], ALU.mult, ALU.add,
                        reads=[("zz", i), ("yy", i), "vecs"], writes=[("yy", i)])
                dve_stt(YY[:, i, 0:N], ZZ[:, i, 2:2 + N], vcol("convw", 2 * 8 + c), YY[:, i, 0:N], ALU.mult, ALU.add,
                        reads=[("zz", i), ("yy", i), "vecs"], writes=[("yy", i)])
                dve_tt(GZ[:, c, 0:N], pg[:, 0:N], YY[:, i, 0:N], ALU.mult, reads=[pgk, ("yy", i)], writes=[("gz", c)])
            for grp in range(2):
                (wv,), wk = wload([cwout[:, grp * 512:(grp + 1) * 512]])
                for j in range(4):
                    oc = grp * 4 + j
                    bank, bk = newbank()
                    mmgroup(bank[:, 0:N], [wv[:, k, j * 128:(j + 1) * 128] for k in range(KC)], [GZ[:, k, 0:N] for k in range(KC)],
                            reads=[wk] + [("gz", k) for k in range(KC)], pskey=bk)
                    resid_update(bank, bk, oc, s, N, Gcol(1, 1, 0, oc), 1, ("GV", 1, 1))
            if tix == len(own_tiles) - 1:
                apply_deferred(s + N, 1)

        fw.barrier()
        if stop_after == 5:
            finish_debug()
            return nc
        f_super = [[("x", OWNOFF, 512), ("x", OWNOFF + 512, 512)], [("x", OWNOFF + 1024, 512), ("x", OWNOFF + 1536, 512)]]
        ffn(1, 1, f_super[0], nxt=(1, 1, f_super[1]))
        ffn(1, 1, f_super[1])

        fw.barrier()
        if stop_after == 6:
            finish_debug()
            return nc
        OS = aview(0, 8 * 512, F32).rearrange("p (c n) -> p c n", c=8)
        oTv = outT.rearrange("(c p) t -> p c t", p=128)
        for (s, N) in split_tiles(OWNOFF, OWNT, tmax=512):
            xk = xblocks(s, N)
            norm_stats(X[:, :, s:s + N], N, xk)
            for c in range(KC):
                i = St.td_i
                St.td_i ^= 1
                dve_tt(tdiv[:, i, 0:N], X[:, c, s:s + N], NS.rinv, ALU.mult, reads=xk + NS.sqkeys, writes=[("td", i)])
                act(OS[:, c, 0:N], tdiv[:, i, 0:N], AF.Identity, reads=[("td", i), "vecs"], writes=["OS"], scale=vcol("fg", c), bias=0.0)
            fw.dma("sp", "st", [(oTv[:, :, s - OWNOFF:s - OWNOFF + N], OS[:, :, 0:N])], reads=["OS"])

        fw.emit()
    return nc


def _fm(v):
    return np.ascontiguousarray(np.asarray(v, np.float32).reshape(-1, 128).T)


def _mask_col(core, m, p, a):
    out = np.zeros(128, np.float32)
    e = 2 * m + a
    gq = 32 * core - 1 + e
    for half in range(2):
        k = 2 * p + half
        gk = 32 * core - 5 + k
        if 0 <= gq < NROWS:
            rs = min(max(gq - 4, 0), NROWS - 8)
        else:
            rs = gq - 4
        valid = (0 <= gk < NROWS) and (rs <= gk < rs + 8)
        out[half * 64:(half + 1) * 64] = 0.0 if valid else NEG
    return out


def _build_vecs(core, norm_g, final_g, pool_scale, conv_w, mod_b):
    LAY = vec_layout()
    v = np.zeros((128, LAY["_n"]), np.float32)
    for l in range(2):
        for s in range(3):
            o = LAY["ng"] + (l * 3 + s) * 8
            v[:, o:o + 8] = _fm(norm_g[l, s])
    v[:, LAY["fg"]:LAY["fg"] + 8] = _fm(final_g)
    v[:, LAY["pscale"]:LAY["pscale"] + 4] = _fm(pool_scale[0])
    for tap in range(3):
        o = LAY["convw"] + tap * 8
        v[:, o:o + 8] = _fm(conv_w[0, tap])
    for l in range(2):
        o = LAY["modb"] + l * 72
        v[:, o:o + 72] = _fm(mod_b[l])
    v[:, LAY["vm"] + 0] = 0.0 if core == 0 else 1.0
    v[:, LAY["vm"] + 1] = 0.0 if core == NCORE - 1 else 1.0
    L = NROWS * GW
    for which in range(2):
        for gi, w in enumerate(POOL_W):
            for i in range(8):
                t = core * OWNT + (i if which == 0 else OWNT - 8 + i)
                lo = min(max(t - w // 2, 0), L)
                hi = min(max(t - w // 2 + w, 0), L)
                v[:, LAY["icnt"] + (which * 4 + gi) * 8 + i] = 1.0 / float(hi - lo)
    ref = None
    for c in range(NCORE):
        for m in range(3, 15):
            cols = [_mask_col(c, m, m + 4, 0), _mask_col(c, m, m + 4, 1), _mask_col(c, m, m, 0), _mask_col(c, m, m, 1)]
            cols = np.stack(cols, 1)
            if ref is None:
                ref = cols
            assert np.array_equal(ref, cols)
            for p in (m + 1, m + 2, m + 3):
                for a in range(2):
                    assert not _mask_col(c, m, p, a).any()
    v[:, LAY["mreg"]:LAY["mreg"] + 4] = ref
    for m in sorted(SPEC_UNITS):
        desc = unit_pairs_desc(m)
        for pi_, p in enumerate(desc):
            for a in range(2):
                v[:, LAY[("mspec", m)] + pi_ * 2 + a] = _mask_col(core, m, p, a)
    qc = np.arange(64)
    cs = np.clip(qc - 8, 0, 48)
    kc = np.arange(64)
    ok = (kc[:, None] >= cs[None, :]) & (kc[:, None] < cs[None, :] + 16)
    cm = np.where(ok, 0.0, NEG).astype(np.float32)
    v[:, LAY["cmask"]:LAY["cmask"] + 64] = np.concatenate([cm, cm], 0)
    return v


def _build_tb(rpb):
    tb = np.zeros((8, 128, 18, 64), np.float32)
    kc = np.arange(64)[:, None]
    qc = np.arange(64)[None, :]
    ci = np.clip(kc - qc + 15, 0, 30)
    for half in range(2):
        for jj in range(18):
            d = 8 - jj + half
            if -7 <= d <= 7:
                tb[:, half * 64:(half + 1) * 64, jj, :] = rpb[:, d + 7][:, ci]
    return np.ascontiguousarray(tb.reshape(8, 128, 1152))


_PROGRAM = None


def kernel(x, c, ctx, c_ctx, mod_w, mod_b, norm_g, ffn_w13, ffn_w2, even_w_in, even_w_out,
           na_rpb, pool_w, pool_scale, conv_w_in, conv_w, conv_w_out, final_g, _dbg=None):
    global _PROGRAM
    f = lambda a: np.ascontiguousarray(np.asarray(a, np.float32))
    x = f(x)[0]
    L = x.shape[0]
    ctxT = np.ascontiguousarray(f(ctx)[0].T)
    cc = np.stack([_fm(f(c)[0]), _fm(f(c_ctx))], -1).reshape(128, 16)
    tb = _build_tb(f(na_rpb)[0])
    shared = {
        "ctxT": ctxT, "cc": np.ascontiguousarray(cc), "tb": tb, "ident": np.eye(128, dtype=np.float32),
        "modw0": f(mod_w[0]), "modw1": f(mod_w[1]),
        "win": f(even_w_in[0]), "wout": f(even_w_out[0]),
        "poolw": np.ascontiguousarray(f(pool_w[0]).reshape(512, 128)),
        "cwin": f(conv_w_in[0]), "cwout": f(conv_w_out[0]),
    }
    for l in range(2):
        for ff in range(2):
            shared["w13_%d%d" % (l, ff)] = f(ffn_w13[l, ff])
            shared["w2_%d%d" % (l, ff)] = f(ffn_w2[l, ff])
    in_maps = []
    for core in range(NCORE):
        t_lo = core * OWNT - 5 * GW
        xe = np.zeros((KVT, D), np.float32)
        a = max(t_lo, 0)
        b = min(t_lo + KVT, L)
        xe[a - t_lo:b - t_lo] = x[a:b]
        m = dict(shared)
        m["xT"] = np.ascontiguousarray(xe.T)
        m["vecs"] = _build_vecs(core, f(norm_g), f(final_g), f(pool_scale), f(conv_w), f(mod_b))
        in_maps.append(m)
    if _dbg is not None:
        stop_after, cores = _dbg
        prog = build_program(stop_after=stop_after)
        res = run_bass_kernel_spmd(prog, [in_maps[i] for i in cores], core_ids=list(range(len(cores))))
        return res.results
    if _PROGRAM is None:
        _PROGRAM = build_program()
    res = run_bass_kernel_spmd(_PROGRAM, in_maps, core_ids=list(range(NCORE)))
    outs = [np.asarray(r["outT"]).T for r in res.results]
    return np.ascontiguousarray(np.concatenate(outs, 0)[None].astype(np.float32))
```

```python
import numpy as np
from contextlib import ExitStack
import concourse.bass as bass
import concourse.mybir as mybir
from concourse.bass_utils import run_bass_kernel_spmd

F32 = mybir.dt.float32
BF16 = mybir.dt.bfloat16
AF = mybir.ActivationFunctionType
ALU = mybir.AluOpType

NCORE = 8
D = 1024
KC = 8
DFF = 2816
GW = 64
NROWS = 256
KVR = 42
KVT = KVR * GW
NPAIR = KVR // 2
EOFF = 4 * GW
ET = 34 * GW
OWNOFF = EOFF + GW
OWNT = 32 * GW
CTXT = 256
NEG = -30000.0
PIPE_LAG = 0
CTX_SHARE = False
POOL_W = (2, 4, 8, 16)

SPEC_UNITS = {0: list(range(0, 7)), 1: list(range(1, 7)), 2: list(range(2, 7)),
              15: list(range(14, 20)), 16: list(range(14, 21))}


def unit_pairs_desc(m):
    if m in SPEC_UNITS:
        return sorted(SPEC_UNITS[m], reverse=True)
    return [m + 4, m + 3, m + 2, m + 1, m]


def tb_index(m, p, a):
    return min(15, max(0, 11 - 2 * (p - m) + a))


def vec_layout():
    lay = {}
    off = 0

    def add(name, n):
        nonlocal off
        lay[name] = off
        off += n

    add("ng", 2 * 3 * 8)
    add("fg", 8)
    add("pscale", 4)
    add("convw", 3 * 8)
    add("modb", 2 * 72)
    add("vm", 2)
    add("icnt", 2 * 4 * 8)
    add("mreg", 4)
    for m in sorted(SPEC_UNITS):
        add(("mspec", m), 2 * len(SPEC_UNITS[m]))
    add("cmask", 64)
    lay["_n"] = off
    return lay


class FW:
    ENGS = ("pe", "act", "dve", "pool", "sp")

    def __init__(self, nc, es):
        self.nc = nc
        self.es = es
        self.sems = {}
        for e in self.ENGS:
            self.sems[e] = es.enter_context(nc.semaphore("sem_" + e))
        self.tick = {e: 0 for e in self.ENGS}
        self.waited = {e: {} for e in self.ENGS}
        self.prog = {e: [] for e in self.ENGS}
        self.res = {}
        self.dma_cnt = {}
        self.n_ins = 0

    def dma_sem(self, key):
        if key not in self.sems:
            self.sems[key] = self.es.enter_context(self.nc.semaphore("sem_" + key))
            self.dma_cnt[key] = 0
        return self.sems[key]

    def _need(self, reads, writes):
        need = {}

        def add(tok):
            if tok is None:
                return
            k, v = tok
            if need.get(k, 0) < v:
                need[k] = v

        for r in reads:
            st = self.res.get(r)
            if st:
                add(st["w"])
        for w in writes:
            st = self.res.get(w)
            if st:
                add(st["w"])
                for t in st["r"]:
                    add(t)
        return need

    def _emit_waits(self, e, need, skip_self=False):
        for k, v in need.items():
            if skip_self and k == e:
                continue
            if self.waited[e].get(k, 0) < v:
                sem = self.sems[k]
                self.prog[e].append(lambda eng, sem=sem, v=v: eng.wait_ge(sem, v))
                self.waited[e][k] = v

    def _record(self, tok, reads, writes):
        for r in reads:
            st = self.res.setdefault(r, {"w": None, "r": []})
            st["r"].append(tok)
            if len(st["r"]) > 24:
                best = {}
                for k, v in st["r"]:
                    if best.get(k, 0) < v:
                        best[k] = v
                st["r"] = list(best.items())
        for w in writes:
            self.res[w] = {"w": tok, "r": []}

    def op(self, e, fn, reads=(), writes=(), inc=True, skip_self=False):
        need = self._need(reads, writes)
        self._emit_waits(e, need, skip_self=skip_self)
        self.n_ins += 1
        if inc:
            sem = self.sems[e]
            self.prog[e].append(lambda eng, fn=fn, sem=sem: fn(eng).then_inc(sem, 1))
            self.tick[e] += 1
            tok = (e, self.tick[e])
        else:
            self.prog[e].append(lambda eng, fn=fn: fn(eng))
            tok = (e, self.tick[e] + 1)
        self._record(tok, reads, writes)
        return tok

    def dma(self, q, semkey, pairs, reads=(), writes=()):
        sem = self.dma_sem(semkey)
        need = self._need(reads, writes)
        self._emit_waits(q, need)
        for (out, in_) in pairs:
            self.dma_cnt[semkey] += 1
            self.prog[q].append(
                lambda eng, out=out, in_=in_, sem=sem: eng.dma_start(out=out, in_=in_).then_inc(sem, 16))
            self.n_ins += 1
        tok = (semkey, 16 * self.dma_cnt[semkey])
        self._record(tok, reads, writes)
        return tok

    def barrier(self, engs=("pe", "act", "dve")):
        snap = {k: self.tick[k] for k in engs}
        for e in engs:
            for k, v in snap.items():
                if k != e and v > 0 and self.waited[e].get(k, 0) < v:
                    sem = self.sems[k]
                    self.prog[e].append(lambda eng, sem=sem, v=v: eng.wait_ge(sem, v))
                    self.waited[e][k] = v

    def emit(self):
        final = {e: self.tick[e] for e in self.ENGS}
        for k, c in self.dma_cnt.items():
            final[k] = 16 * c
        for e in self.ENGS:
            for k, v in final.items():
                if v > 0 and k != e and self.waited[e].get(k, 0) < v:
                    sem = self.sems[k]
                    self.prog[e].append(lambda eng, sem=sem, v=v: eng.wait_ge(sem, v))
        prog = self.prog
        with self.nc.Block() as block:
            @block.tensor
            def _(eng):
                for c in prog["pe"]:
                    c(eng)

            @block.scalar
            def _(eng):
                for c in prog["act"]:
                    c(eng)

            @block.vector
            def _(eng):
                for c in prog["dve"]:
                    c(eng)

            @block.gpsimd
            def _(eng):
                for c in prog["pool"]:
                    c(eng)

            @block.sync
            def _(eng):
                for c in prog["sp"]:
                    c(eng)


def split_tiles(t0, total, tmax=512, align=128):
    out = []
    pos = 0
    while pos < total:
        n = min(tmax, total - pos)
        out.append((t0 + pos, n))
        pos += n
    return out


def build_program(stop_after=None):
    nc = bass.Bass("TRN2", target_bir_lowering=False)
    LAY = vec_layout()
    NV = LAY["_n"]

    def din(name, shape):
        return nc.dram_tensor(name, list(shape), F32, kind="ExternalInput").ap()

    xT = din("xT", [D, KVT])
    ctxT = din("ctxT", [D, CTXT])
    cc_d = din("cc", [128, 16])
    vecs_d = din("vecs", [128, NV])
    tb_d = din("tb", [8, 128, 1152])
    ident_d = din("ident", [128, 128])
    modw = [din("modw%d" % l, [D, 9 * D]) for l in range(2)]
    w13 = [[din("w13_%d%d" % (l, f), [D, 2 * DFF]) for f in range(2)] for l in range(2)]
    w2 = [[din("w2_%d%d" % (l, f), [DFF, D]) for f in range(2)] for l in range(2)]
    w_in = din("win", [D, 2048])
    w_out = din("wout", [D, D])
    poolw = din("poolw", [512, 128])
    cwin = din("cwin", [D, 3 * D])
    cwout = din("cwout", [D, D])
    outT = nc.dram_tensor("outT", [D, OWNT], F32, kind="ExternalOutput").ap()
    if stop_after is not None:
        xdbg = nc.dram_tensor("xdbg", [D, KVT], F32, kind="ExternalOutput").ap()
        mdbg = nc.dram_tensor("mdbg", [128, 288], F32, kind="ExternalOutput").ap()
        kdbg = nc.dram_tensor("kdbg", [128, 4 * KVT], BF16, kind="ExternalOutput").ap()
        vdbg = nc.dram_tensor("vdbg", [128, NPAIR * 512], BF16, kind="ExternalOutput").ap()

    es = ExitStack()
    with es:
        fw = FW(nc, es)

        def sb(name, shape, dt):
            return es.enter_context(nc.sbuf_tensor("sb_" + name, list(shape), dt))

        X = sb("X", [128, KC, KVT], F32)
        Kt = sb("Kt", [128, 4, KVT], BF16)
        Vt = sb("Vt", [128, NPAIR, 512], BF16)
        Kc = sb("Kc", [128, 4, CTXT], BF16)
        Vc = sb("Vc", [128, 2, 512], BF16)
        wring = [sb("wring%d" % i, [128, 4096], BF16) for i in range(2)]
        XNW = 896
        xn = sb("xn", [128, KC, XNW], BF16)
        tdiv = sb("tdiv", [128, 2, 512], F32)
        TBall = sb("TBall", [128, 2112], F32)
        TBv = TBall[:, :].bitcast(BF16)
        TBb = [TBv[:, i * 1408:i * 1408 + 1152] for i in range(3)]
        TBm = [TBv[:, i * 1408 + 1152:i * 1408 + 1408] for i in range(3)]
        TBs = TBb
        identb = sb("identb", [128, 128], BF16)
        PWB = sb("PWB", [128, 4, 128], BF16)
        wring.append(TBv[:, 0:4096])
        vecs = sb("vecs", [128, NV], F32)
        ccs = sb("ccs", [128, 16], F32)
        scc = sb("scc", [128, 16], BF16)
        Msb = [sb("Msb%d" % l, [128, 72, 2], F32) for l in range(2)]
        AV = sb("AV", [128, 2 * 3 * 2 * 8], F32)
        GV = sb("GV", [128, 2 * 3 * 2 * 8], F32)
        ones = sb("ones", [128, 128], BF16)
        YH = sb("YH", [128, KC, 8], F32)
        ARENA_B = 27904
        arena = sb("arena", [128, ARENA_B // 2], BF16)
        psb = [es.enter_context(nc.psum_tensor("ps%d" % i, [128, 512], F32)) for i in range(8)]

        class St:
            xn = None
            G = None
            cap = 896
            pre = None
            nw = 3
            nbanks = 7
            bank = 0
            wslot = 0
            sa_i = 0
            td_i = 0

        def aview(boff, nelem, dt, base=None, cap=None):
            assert boff % 4 == 0
            if base is None:
                base, cap = arena, ARENA_B
            if dt == BF16:
                assert boff + 2 * nelem <= cap, (boff, nelem)
                return base[:, boff // 2: boff // 2 + nelem]
            assert boff + 4 * nelem <= cap, (boff, nelem)
            return base[:, boff // 2: boff // 2 + 2 * nelem].bitcast(F32)

        def newbank():
            i = St.bank
            St.bank = (i + 1) % St.nbanks
            return psb[i], ("ps", i)

        def xblocks(t0, n):
            return [("X", b) for b in range(t0 // 64, (t0 + n - 1) // 64 + 1)]

        def vcol(name, idx=0, n=1):
            o = LAY[name] + idx
            return vecs[:, o:o + n]

        def avi(l, s, col, fc):
            return ((l * 3 + s) * 2 + col) * 8 + fc

        def Acol(l, s, col, fc):
            i = avi(l, s, col, fc)
            return AV[:, i:i + 1]

        def Gcol(l, s, col, fc):
            i = avi(l, s, col, fc)
            return GV[:, i:i + 1]

        def Bcol(l, s, col, fc):
            return Msb[l][:, (3 * s) * 8 + fc, col:col + 1]

        def mmgroup(out_ap, lhs_list, rhs_list, reads, pskey):
            n = len(lhs_list)
            for i in range(n):
                fw.op("pe", lambda e, o=out_ap, a=lhs_list[i], b=rhs_list[i], st=(i == 0), sp=(i == n - 1):
                      e.matmul(o, lhsT=a, rhs=b, start=st, stop=sp),
                      reads=reads, writes=[pskey], inc=(i == n - 1), skip_self=True)

        def wload(parts):
            slot = St.wslot
            St.wslot = (slot + 1) % St.nw
            off = 0
            views = []
            pairs = []
            for ap in parts:
                K, ncols = ap.shape
                kc = K // 128
                assert off + kc * ncols <= 4096
                view = wring[slot][:, off: off + kc * ncols].rearrange("p (k m) -> p k m", k=kc)
                pairs.append((view, ap.rearrange("(k p) m -> p k m", p=128)))
                views.append(view)
                off += kc * ncols
            key = ("w", slot)
            fw.dma("pool", "w%d" % slot, pairs,
                   writes=[key] + ([("tb", i) for i in range(3)] + [("tbm", i) for i in range(3)] if slot == 2 else []))
            return views, key

        def act(out, in_, func, reads, writes, **kw):
            fw.op("act", lambda e: e.activation(out=out, in_=in_, func=func, **kw), reads=reads, writes=writes)

        def dve_tt(out, in0, in1, op, reads, writes):
            fw.op("dve", lambda e: e.tensor_tensor(out=out, in0=in0, in1=in1, op=op), reads=reads, writes=writes)

        def dve_stt(out, in0, scalar, in1, op0, op1, reads, writes):
            fw.op("dve", lambda e: e.scalar_tensor_tensor(out=out, in0=in0, scalar=scalar, in1=in1, op0=op0, op1=op1),
                  reads=reads, writes=writes)

        def dve_ts(out, in0, s1, s2, op0, op1, reads, writes):
            if s2 is None:
                fw.op("dve", lambda e: e.tensor_scalar(out=out, in0=in0, scalar1=s1, scalar2=None, op0=op0),
                      reads=reads, writes=writes)
            else:
                fw.op("dve", lambda e: e.tensor_scalar(out=out, in0=in0, scalar1=s1, scalar2=s2, op0=op0, op1=op1),
                      reads=reads, writes=writes)

        def dve_copy(out, in_, reads, writes):
            fw.op("dve", lambda e: e.tensor_copy(out=out, in_=in_), reads=reads, writes=writes)

        def resid_update(bank, bk, oc, s, N, gcol, defer, gkey):
            main = N - defer
            xk = xblocks(s, main)
            dve_stt(X[:, oc, s:s + main], bank[:, 0:main], gcol, X[:, oc, s:s + main], ALU.mult, ALU.add,
                    reads=[bk, gkey] + xk, writes=xk)
            if defer:
                dve_ts(YH[:, oc, 0:defer], bank[:, main:N], gcol, None, ALU.mult, None, reads=[bk, gkey], writes=["YH"])

        def apply_deferred(end, defer):
            xk = xblocks(end - defer, defer)
            dve_tt(X[:, :, end - defer:end], X[:, :, end - defer:end], YH[:, :, 0:defer], ALU.add, reads=["YH"] + xk, writes=xk)

        SQ_OFF = 19712

        class NS:
            sqoff = SQ_OFF
            sqkeys = ["sq"]
            rinv = None
            base = None
            cap = None

        def norm_stats(x3, n, xkeys):
            sk = NS.sqkeys
            sqv = aview(NS.sqoff, 8 * n, BF16, NS.base, NS.cap).rearrange("p (c n) -> p c n", c=8)
            sd = aview(NS.sqoff, n, F32, NS.base, NS.cap)
            rinv = aview(NS.sqoff + 4 * n, n, F32, NS.base, NS.cap)
            NS.rinv = rinv
            act(sqv, x3, AF.Square, reads=xkeys, writes=sk)
            bank, bk = newbank()
            mmgroup(bank[:, 0:n], [ones[:, :]] * KC, [sqv[:, c, :] for c in range(KC)], reads=sk + ["ones"], pskey=bk)
            act(sd, bank[:, 0:n], AF.Sqrt, reads=[bk], writes=sk, scale=1.0 / D, bias=1e-6)
            fw.op("dve", lambda e: e.reciprocal(out=rinv, in_=sd), reads=sk, writes=sk)

        def norm_mod(x3, n, xkeys, l, s, col, dst3, dkeys):
            norm_stats(x3, n, xkeys)
            for c in range(KC):
                i = St.td_i
                St.td_i ^= 1
                dve_tt(tdiv[:, i, 0:n], x3[:, c, :], NS.rinv, ALU.mult, reads=xkeys + NS.sqkeys, writes=[("td", i)])
                act(dst3[:, c, :], tdiv[:, i, 0:n], AF.Identity, reads=[("td", i), ("AV", l, s), ("M", l, s)], writes=dkeys,
                    scale=Acol(l, s, col, c), bias=Bcol(l, s, col, c))

        def norm_sq(x3, n, xkeys):
            sqv = aview(NS.sqoff, 8 * n, BF16, NS.base, NS.cap).rearrange("p (c n) -> p c n", c=8)
            act(sqv, x3, AF.Square, reads=xkeys, writes=NS.sqkeys)

        def norm_fin(x3, n, xkeys, l, s, col, dst3, dkeys):
            sk = NS.sqkeys
            sqv = aview(NS.sqoff, 8 * n, BF16, NS.base, NS.cap).rearrange("p (c n) -> p c n", c=8)
            sd = aview(NS.sqoff, n, F32, NS.base, NS.cap)
            rinv = aview(NS.sqoff + 4 * n, n, F32, NS.base, NS.cap)
            bank, bk = newbank()
            mmgroup(bank[:, 0:n], [ones[:, :]] * KC, [sqv[:, c, :] for c in range(KC)], reads=sk + ["ones"], pskey=bk)
            act(sd, bank[:, 0:n], AF.Sqrt, reads=[bk], writes=sk, scale=1.0 / D, bias=1e-6)
            fw.op("dve", lambda e: e.reciprocal(out=rinv, in_=sd), reads=sk, writes=sk)
            for c in range(KC):
                i = St.td_i
                St.td_i ^= 1
                dve_tt(tdiv[:, i, 0:n], x3[:, c, :], rinv, ALU.mult, reads=xkeys + sk, writes=[("td", i)])
                act(dst3[:, c, :], tdiv[:, i, 0:n], AF.Identity, reads=[("td", i), ("AV", l, s), ("M", l, s)], writes=dkeys,
                    scale=Acol(l, s, col, c), bias=Bcol(l, s, col, c))

        G_ = aview(0, 11 * 896, BF16).rearrange("p (c n) -> p c n", c=11)

        St.xn = xn
        St.G = G_

        class Hook:
            side = []
            every = 1
            cnt = 0

        def run_side(force_all=False):
            if force_all:
                while Hook.side:
                    Hook.side.pop(0)()
                return
            Hook.cnt += 1
            if Hook.side and Hook.cnt % Hook.every == 0:
                Hook.side.pop(0)()

        def tbuf(kind):
            return X if kind == "x" else Xctx

        def tkeys(kind, t0, n):
            return xblocks(t0, n) if kind == "x" else ["Xctx"]

        def ffn(l, f, tiles, nxt=None):
            s = 0 if f == 0 else 2
            offs = []
            o = 0
            for (kind, t0, n) in tiles:
                offs.append(o)
                o += n
            assert o <= St.cap
            xn = St.xn
            G_ = St.G
            if St.pre is not None:
                assert St.pre == (l, f, tuple(tiles)), (St.pre, l, f, tiles)
                St.pre = None
            else:
                for ti, (kind, t0, n) in enumerate(tiles):
                    norm_mod(tbuf(kind)[:, :, t0:t0 + n], n, tkeys(kind, t0, n), l, s, 0 if kind == "x" else 1,
                             xn[:, :, offs[ti]:offs[ti] + n], [("xn", ti)])
            stages = []
            if nxt is not None:
                l2, f2, tiles2 = nxt
                assert [t[2] for t in tiles2][:-1] == [t[2] for t in tiles][:-1] and tiles2[-1][2] <= tiles[-1][2]
                s2 = 0 if f2 == 0 else 2
                nargs = [(tbuf(kind)[:, :, t0:t0 + n], n, tkeys(kind, t0, n), l2, s2, 0 if kind == "x" else 1,
                          xn[:, :, offs[ti]:offs[ti] + n], [("xn", ti)]) for ti, (kind, t0, n) in enumerate(tiles2)]

                def stage(k):
                    if k >= 1:
                        norm_fin(*nargs[k - 1])
                    if k < len(nargs):
                        norm_sq(*nargs[k][:3])
                stages = [lambda k=k: stage(k) for k in range(len(nargs) + 1)]
                assert len(stages) <= 5
                St.pre = (l2, f2, tuple(tiles2))
            wA = w13[l][f]
            wB = w2[l][f]
            for half in range(2):
                for cp in range(0, 11, 2):
                    cls = [c for c in (cp, cp + 1) if c < 11]
                    ncl = len(cls)
                    c0 = half * 11 + cp
                    (wa, wb), wk = wload([wA[:, c0 * 128:(c0 + ncl) * 128], wA[:, DFF + c0 * 128: DFF + (c0 + ncl) * 128]])
                    for j, cl in enumerate(cls):
                        for ti, (kind, t0, n) in enumerate(tiles):
                            xo = offs[ti]
                            pa, pak = newbank()
                            pb, pbk = newbank()
                            rhs = [xn[:, k, xo:xo + n] for k in range(KC)]
                            mmgroup(pa[:, 0:n], [wa[:, k, j * 128:(j + 1) * 128] for k in range(KC)], rhs, reads=[wk, ("xn", ti)], pskey=pak)
                            mmgroup(pb[:, 0:n], [wb[:, k, j * 128:(j + 1) * 128] for k in range(KC)], rhs, reads=[wk, ("xn", ti)], pskey=pbk)
                            i = St.td_i
                            St.td_i ^= 1
                            act(tdiv[:, i, 0:n], pa[:, 0:n], AF.Silu, reads=[pak], writes=[("td", i)])
                            dve_tt(G_[:, cl, xo:xo + n], pb[:, 0:n], tdiv[:, i, 0:n], ALU.mult, reads=[pbk, ("td", i)], writes=[("g", cl, ti)])
                    run_side()
                if half == 1 and stages:
                    stages.pop(0)()
                for oc0 in range(0, 8, 2):
                    (wv,), wk = wload([wB[half * 1408:(half + 1) * 1408, oc0 * 128:(oc0 + 2) * 128]])
                    for j in range(2):
                        oc = oc0 + j
                        for ti, (kind, t0, n) in enumerate(tiles):
                            xo = offs[ti]
                            col = 0 if kind == "x" else 1
                            xbuf = tbuf(kind)
                            py, pyk = newbank()
                            mmgroup(py[:, 0:n], [wv[:, k, j * 128:(j + 1) * 128] for k in range(11)],
                                    [G_[:, k, xo:xo + n] for k in range(11)],
                                    reads=[wk] + [("g", k, ti) for k in range(11)], pskey=pyk)
                            xk = tkeys(kind, t0, n)
                            dve_stt(xbuf[:, oc, t0:t0 + n], py[:, 0:n], Gcol(l, s, col, oc), xbuf[:, oc, t0:t0 + n], ALU.mult, ALU.add,
                                    reads=[pyk, ("GV", l, s)] + xk, writes=xk)
                    run_side()
                    if half == 1 and stages:
                        stages.pop(0)()
            assert not stages

        def kv_project(tiles):
            offs = []
            o = 0
            for (kind, t0, n) in tiles:
                offs.append(o)
                o += n
            xn = aview(0, 8 * XNW, BF16).rearrange("p (c n) -> p c n", c=8)
            allg = [("g", k, t) for k in range(11) for t in range(3)]
            for ti, (kind, t0, n) in enumerate(tiles):
                norm_mod(tbuf(kind)[:, :, t0:t0 + n], n, tkeys(kind, t0, n), 0, 1, 0 if kind == "x" else 1,
                         xn[:, :, offs[ti]:offs[ti] + n], [("kvxn", ti)] + allg)
            (wv,), wk = wload([w_in[:, 512:1024]])
            for oc in range(4):
                for ti, (kind, t0, n) in enumerate(tiles):
                    xo = offs[ti]
                    bank, bk = newbank()
                    mmgroup(bank[:, 0:n], [wv[:, k, oc * 128:(oc + 1) * 128] for k in range(KC)], [xn[:, k, xo:xo + n] for k in range(KC)],
                            reads=[wk, ("kvxn", ti)] + allg, pskey=bk)
                    if kind == "x":
                        act(Kt[:, oc, t0:t0 + n], bank[:, 0:n], AF.Copy, reads=[bk],
                            writes=[("K", b) for b in range(t0 // 128, (t0 + n) // 128)] + ["Xctx"])
                    else:
                        act(Kc[:, oc, t0:t0 + n], bank[:, 0:n], AF.Copy, reads=[bk], writes=["Kc"])
            (wv,), wk = wload([w_in[:, 1024:1536]])
            for ti, (kind, t0, n) in enumerate(tiles):
                xo = offs[ti]
                for b in range(n // 128):
                    bank, bk = newbank()
                    mmgroup(bank[:, 0:512], [xn[:, k, xo + b * 128: xo + (b + 1) * 128] for k in range(KC)], [wv[:, k, 0:512] for k in range(KC)],
                            reads=[wk, ("kvxn", ti)] + allg, pskey=bk)
                    pr = t0 // 128 + b
                    if kind == "x":
                        dve_copy(Vt[:, pr, :], bank[:, 0:512], reads=[bk], writes=[("V", pr)])
                    else:
                        dve_copy(Vc[:, pr, :], bank[:, 0:512], reads=[bk], writes=["Vc"])

        def finish_debug():
            allx = [("X", b) for b in range(KVT // 64)]
            fw.dma("sp", "dbg", [(xdbg.rearrange("(c p) t -> p c t", p=128), X[:, :, :])], reads=allx)
            fw.dma("sp", "dbg", [(mdbg[:, 0:144], Msb[0][:, :, :].rearrange("p j t -> p (j t)")),
                                 (mdbg[:, 144:288], Msb[1][:, :, :].rearrange("p j t -> p (j t)"))],
                   reads=[("M", l, s_) for l in range(2) for s_ in range(3)])
            fw.dma("sp", "dbg", [(kdbg[:, :], Kt[:, :, :].rearrange("p c n -> p (c n)")),
                                 (vdbg[:, :], Vt[:, :, :].rearrange("p c n -> p (c n)"))],
                   reads=[("K", b) for b in range(NPAIR)] + [("V", b) for b in range(NPAIR)])
            fw.emit()

        xTv = xT.rearrange("(c p) t -> p c t", p=128)
        fw.dma("sp", "ld0", [(vecs[:, :], vecs_d[:, :]), (ccs[:, :], cc_d[:, :])], writes=["vecs", "ccs"])
        fw.op("dve", lambda e: e.memset(ones[:, :], 1.0), writes=["ones"])
        for c in range(KC):
            fw.dma("sp" if c % 2 == 0 else "act", "ldx%d" % c, [(X[:, c, :], xTv[:, c, :])], writes=[("Xc_", c)])
        fw.op("dve", lambda e: e.memset(tdiv[:, 0, 0:1], 0.0), reads=[("Xc_", c) for c in range(KC)], writes=[("X", b) for b in range(KVT // 64)] + [("td", 0)])
        act(scc[:, :], ccs[:, :], AF.Silu, reads=["ccs"], writes=["scc"])
        fw.dma("sp", "ld0", [(tdiv[:, 1, 0:128], ident_d[:, :])], writes=[("td", 1)])
        act(identb[:, :], tdiv[:, 1, 0:128], AF.Copy, reads=[("td", 1)], writes=["identb"])
        scc3 = scc[:, :].rearrange("p (k t) -> p k t", t=2)

        Xctx = Kt[:, :, :].rearrange("p c n -> p (c n)")[:, 0:2 * KC * CTXT].bitcast(F32).rearrange("p (c n) -> p c n", c=KC)
        cT = ctxT.rearrange("(c p) t -> p c t", p=128)
        fw.dma("sp", "ldc", [(Xctx[:, :, :], cT)], writes=["Xctx"])

        mbank = psb[7]
        mvs = [mbank[:, l * 144:(l + 1) * 144].rearrange("p (j t) -> p j t", t=2) for l in range(2)]

        def adaln_group(l, grp):
            mv = mvs[l]
            (wv,), wk = wload([modw[l][:, grp * 512:(grp + 1) * 512]])
            for j in range(4):
                jf = grp * 4 + j
                for k in range(KC):
                    fw.op("pe", lambda e, o=mv[:, jf, :], a=wv[:, k, j * 128:(j + 1) * 128], b=scc3[:, k, :], st=(k == 0), sp=(k == KC - 1):
                          e.matmul(o, lhsT=a, rhs=b, start=st, stop=sp),
                          reads=[wk, "scc"], writes=[("mps", l, jf // 24)], inc=(k == KC - 1), skip_self=True)

        def adaln_final(l, s_):
            mv = mvs[l]
            mb = vcol("modb", l * 72 + 24 * s_, 24)
            dve_tt(Msb[l][:, 24 * s_:24 * s_ + 24, :], mv[:, 24 * s_:24 * s_ + 24, :], mb.unsqueeze(2).to_broadcast([128, 24, 2]), ALU.add,
                   reads=[("mps", l, s_), "vecs"], writes=[("M", l, s_)])
            for col in range(2):
                i0 = avi(l, s_, col, 0)
                ngv = vcol("ng", (l * 3 + s_) * 8, 8)
                dve_stt(AV[:, i0:i0 + 8], Msb[l][:, (3 * s_ + 1) * 8:(3 * s_ + 2) * 8, col], 1.0, ngv, ALU.add, ALU.mult,
                        reads=[("M", l, s_), "vecs"], writes=[("AV", l, s_)])
                dve_ts(GV[:, i0:i0 + 8], Msb[l][:, (3 * s_ + 2) * 8:(3 * s_ + 3) * 8, col], 0.5 if s_ != 1 else 1.0, None, ALU.mult, None,
                       reads=[("M", l, s_)], writes=[("GV", l, s_)])

        for grp in range(6):
            adaln_group(0, grp)
        adaln_final(0, 0)
        for s_ in (1, 2):
            for grp in range(6 * s_, 6 * s_ + 6):
                Hook.side.append(lambda grp=grp: adaln_group(0, grp))
            Hook.side.append(lambda s_=s_: adaln_final(0, s_))
        side_l1 = []
        for s_ in range(3):
            for grp in range(6 * s_, 6 * s_ + 6):
                side_l1.append(lambda grp=grp: adaln_group(1, grp))
            side_l1.append(lambda s_=s_: adaln_final(1, s_))

        p1_super = [[("x", 0, 512), ("c", 0, 256)],
                    [("x", 512, 512), ("x", 1024, 256)],
                    [("x", 1280, 512), ("x", 1792, 256)],
                    [("x", 2048, 512), ("x", 2560, 128)]]
        for sti, tiles in enumerate(p1_super):
            if sti == 1:
                Hook.side.extend(side_l1)
                Hook.every = 3
            ffn(0, 0, tiles, nxt=(0, 0, p1_super[sti + 1]) if sti + 1 < len(p1_super) else None)
            if sti == 0:
                run_side(force_all=True)
            kv_project(tiles)
        run_side(force_all=True)
        Hook.every = 1
        St.nbanks = 8

        fw.barrier()
        if stop_after == 1:
            finish_debug()
            return nc
        QT = aview(0, 4 * 256, BF16).rearrange("p (c n) -> p c n", c=4)
        U = aview(2048, 4 * 272, F32).rearrange("p (c n) -> p c n", c=4)
        PT_ = [aview(6400 + i * 1088, 272, F32) for i in range(2)]
        DD = aview(8576, 4 * 256, BF16).rearrange("p (c n) -> p c n", c=4)
        MIX = aview(10624, 8 * 256, BF16).rearrange("p (c n) -> p c n", c=8)
        MSP = [aview(14720 + i * 1792, 7 * 128, BF16).rearrange("p (s q) -> p s q", q=128) for i in range(2)]
        PTb = [aview(21888 + i * 2304, 9 * 128, BF16) for i in range(2)]
        RD = aview(26496, 2 * 128, F32).rearrange("p (i n) -> p i n", i=2)
        T8 = aview(27520, 8, F32)
        cmask = vcol("cmask", 0, 64)
        fw.dma("pool", "pwb", [(PWB[:, :, :], poolw.rearrange("(k p) m -> p k m", p=128))], writes=["PWB"])

        def mixer0_tile(e0, N):
            s = EOFF + e0
            n2 = N + 16
            norm_mod(X[:, :, s - 8:s + N + 8], n2, xblocks(s - 8, n2), 0, 1, 0, xn[:, :, 0:n2], [("xn", 0)])
            if e0 > 0:
                apply_deferred(s, 8)
            (wv,), wk = wload([w_in[:, 1536:2048]])
            for gi in range(4):
                bank, bk = newbank()
                mmgroup(bank[:, 0:n2], [wv[:, k, gi * 128:(gi + 1) * 128] for k in range(KC)], [xn[:, k, 0:n2] for k in range(KC)],
                        reads=[wk, ("xn", 0)], pskey=bk)
                act(U[:, gi, 0:n2], bank[:, 0:n2], AF.Copy, reads=[bk], writes=["U"])
            (wv,), wk = wload([w_in[:, 0:512]])
            for oc in range(4):
                bank, bk = newbank()
                mmgroup(bank[:, 0:N], [wv[:, k, oc * 128:(oc + 1) * 128] for k in range(KC)], [xn[:, k, 8:8 + N] for k in range(KC)],
                        reads=[wk, ("xn", 0)], pskey=bk)
                act(QT[:, oc, 0:N], bank[:, 0:N], AF.Copy, reads=[bk], writes=["QT"], scale=0.125)
            wo = [wload([w_out[:, grp * 512:(grp + 1) * 512]]) for grp in range(2)]
            for which, (lo, hi) in enumerate(((56, 64), (2112, 2120))):
                a = max(lo, e0 - 8)
                b = min(hi, e0 + N + 8)
                if a < b:
                    ca = a - (e0 - 8)
                    cb = b - (e0 - 8)
                    dve_ts(U[:, :, ca:cb], U[:, :, ca:cb], vcol("vm", which), None, ALU.mult, None, reads=["U", "vecs"], writes=["U"])
            for gi, w in enumerate(POOL_W):
                cur = U[:, gi, 0:n2]
                width = n2
                step = 1
                pi = 0
                rk = ["U"]
                while step < w:
                    nw_ = width - step
                    dst = PT_[pi][:, 0:nw_]
                    dve_tt(dst, cur[:, 0:nw_], cur[:, step:step + nw_], ALU.add, reads=rk, writes=[("pt", pi)])
                    rk = [("pt", pi)]
                    cur = PT_[pi][:, 0:nw_]
                    pi ^= 1
                    width = nw_
                    step *= 2
                c0 = 8 - w // 2
                dve_stt(DD[:, gi, 0:N], cur[:, c0:c0 + N], 1.0 / w, U[:, gi, 8:8 + N], ALU.mult, ALU.subtract,
                        reads=rk + ["U"], writes=["DD"])
                for which, lo in enumerate((64, 2104)):
                    if e0 <= lo and lo + 8 <= e0 + N:
                        i0 = lo - e0
                        ic = vcol("icnt", (which * 4 + gi) * 8, 8)
                        dve_tt(T8[:, 0:8], cur[:, c0 + i0:c0 + i0 + 8], ic, ALU.mult, reads=rk + ["vecs"], writes=["T8"])
                        dve_tt(DD[:, gi, i0:i0 + 8], T8[:, 0:8], U[:, gi, 8 + i0:8 + i0 + 8], ALU.subtract, reads=["T8", "U"], writes=["DD"])
            units = []
            for h in range(8):
                for m in range(e0 // 128, (e0 + N) // 128):
                    units.append((h, m))

            def emit_tb_dma(h):
                slot = h % 3
                fw.dma("pool", "tb%d" % slot, [(TBb[slot], tb_d[h])], writes=[("tb", slot), ("tbm", slot), ("w", 2)])

            def emit_tb_prep(h):
                slot = h % 3
                t3 = TBb[slot].rearrange("p (j q) -> p j q", q=64)
                dve_tt(t3, t3, cmask.unsqueeze(1).to_broadcast([128, 18, 64]), ALU.add, reads=[("tb", slot), "vecs"], writes=[("tb", slot)])
                for idx, (jj, mi) in enumerate(((4, 0), (5, 1), (12, 2), (13, 3))):
                    dve_ts(TBm[slot][:, idx * 64:(idx + 1) * 64], TBb[slot][:, jj * 64:(jj + 1) * 64], vcol("mreg", mi), None, ALU.add, None,
                           reads=[("tb", slot), "vecs"], writes=[("tbm", slot)])

            spec_in_tile = [m for m in range(e0 // 128, (e0 + N) // 128) if m in SPEC_UNITS]
            for idx, m in enumerate(spec_in_tile):
                desc = unit_pairs_desc(m)
                for pi_ in range(len(desc)):
                    for a in range(2):
                        act(MSP[idx][:, pi_, a * 64:(a + 1) * 64], cmask, AF.Identity, reads=["vecs"], writes=[("msp", idx)],
                            scale=0.0, bias=vcol(("mspec", m), pi_ * 2 + a))

            def emit_qk(ui):
                h, m = units[ui]
                desc = unit_pairs_desc(m)
                special = m in SPEC_UNITS
                hp = (h % 2) * 64
                hc = h // 2
                qc0 = m * 128 - e0
                q = QT[hp:hp + 64, hc, qc0:qc0 + 128]
                banks = [newbank(), newbank()]
                if special or not CTX_SHARE:
                    cbank, cbk = newbank()
                    ccol = 0
                else:
                    cbank, cbk = banks[1]
                    ccol = 128
                slot = h % 3
                for pi_, p in enumerate(desc):
                    bank, bk = banks[pi_ // 4]
                    col = (pi_ % 4) * 128
                    fw.op("pe", lambda e, o=bank[:, col:col + 128], a=Kt[hp:hp + 64, hc, p * 128:(p + 1) * 128], b=q, st=(pi_ % 4 == 0):
                          e.matmul(o, lhsT=a, rhs=b, start=st, stop=False, skip_group_check=True),
                          reads=[("K", p), "QT"], writes=[bk], inc=False, skip_self=True)
                for pi_, p in enumerate(desc):
                    bank, bk = banks[pi_ // 4]
                    col = (pi_ % 4) * 128
                    if special:
                        jj0 = 12 - 2 * (p - m)
                        assert 0 <= jj0 <= 16
                        bt = TBb[slot][:, jj0 * 64:(jj0 + 2) * 64]
                    elif pi_ == 0:
                        bt = TBm[slot][:, 0:128]
                    elif pi_ == 4:
                        bt = TBm[slot][:, 128:256]
                    else:
                        bt = TBb[slot][:, (4 + 2 * pi_) * 64:(6 + 2 * pi_) * 64]
                    fw.op("pe", lambda e, o=bank[:, col:col + 128], b=bt:
                          e.matmul(o, lhsT=identb[:, :], rhs=b, start=False, stop=True, skip_group_check=True),
                          reads=[("tb", slot), ("tbm", slot), "identb"], writes=[bk], inc=False, skip_self=True)
                    if special:
                        sidx = spec_in_tile.index(m)
                        fw.op("pe", lambda e, o=bank[:, col:col + 128], b=MSP[sidx][:, pi_, :]:
                              e.matmul(o, lhsT=identb[:, :], rhs=b, start=False, stop=True, skip_group_check=True),
                              reads=[("msp", sidx), "identb"], writes=[bk], inc=False, skip_self=True)
                for b_ in range(2):
                    fw.op("pe", lambda e, o=cbank[:, ccol + b_ * 128:ccol + (b_ + 1) * 128], a=Kc[hp:hp + 64, hc, b_ * 128:(b_ + 1) * 128], b=q:
                          e.matmul(o, lhsT=a, rhs=b, start=True, stop=True),
                          reads=["Kc", "QT"], writes=[cbk], inc=(b_ == 1), skip_self=True)
                return dict(h=h, m=m, desc=desc, banks=banks, cbank=(cbank, cbk), ccol=ccol, ui=ui)

            def emit_soft(stt):
                h, m, desc, ui = stt["h"], stt["m"], stt["desc"], stt["ui"]
                npos = len(desc)
                P = PTb[ui % 2]
                pk = ("P", ui % 2)
                (ba, bak), (bb, bbk) = stt["banks"]
                act(P[:, 0:512], ba[:, 0:512], AF.Exp, reads=[bak], writes=[pk])
                act(P[:, 512:npos * 128], bb[:, 0:(npos - 4) * 128], AF.Exp, reads=[bbk], writes=[pk])
                cbank, cbk = stt["cbank"]
                cc_ = stt["ccol"]
                act(P[:, npos * 128:(npos + 2) * 128], cbank[:, cc_:cc_ + 256], AF.Exp, reads=[cbk], writes=[pk])

            def emit_pv(stt):
                h, m, desc, ui = stt["h"], stt["m"], stt["desc"], stt["ui"]
                hp = (h % 2) * 64
                hc = h // 2
                qc0 = m * 128 - e0
                npos = len(desc)
                P = PTb[ui % 2]
                pk = ("P", ui % 2)
                ob, obk = newbank()
                lhs = [Vt[:, p, h * 64:(h + 1) * 64] for p in desc] + [Vc[:, b_, h * 64:(h + 1) * 64] for b_ in range(2)]
                rhs = [P[:, i * 128:(i + 1) * 128] for i in range(npos + 2)]
                rd = [("V", p) for p in desc] + ["Vc", pk]
                mmgroup(ob[hp:hp + 64, 0:128], lhs, rhs, reads=rd, pskey=obk)
                mmgroup(ob[hp:hp + 64, 128:256], [ones[:, 0:64]] * (npos + 2), rhs, reads=[pk, "ones"], pskey=obk)
                ri = ui % 2
                fw.op("dve", lambda e: e.reciprocal(out=RD[hp:hp + 64, ri, :], in_=ob[hp:hp + 64, 128:256]), reads=[obk], writes=[("rd", ri)])
                dve_tt(MIX[hp:hp + 64, hc, qc0:qc0 + 128], ob[hp:hp + 64, 0:128], RD[hp:hp + 64, ri, :], ALU.mult,
                       reads=[obk, ("rd", ri)], writes=[("mix", hc)])

            upr = len(units) // 8
            infl = []
            emit_tb_dma(0)
            emit_tb_dma(1)
            emit_tb_prep(0)
            for ui in range(len(units)):
                h = units[ui][0]
                first = (ui % upr == 0)
                infl.append(emit_qk(ui))
                if first and h + 2 < 8:
                    emit_tb_dma(h + 2)
                if ui >= 1:
                    emit_soft(infl[ui - 1])
                    emit_pv(infl[ui - 1])
                if first and h + 1 < 8:
                    emit_tb_prep(h + 1)
            emit_soft(infl[-1])
            emit_pv(infl[-1])
            for gi in range(4):
                bank, bk = newbank()
                mmgroup(bank[:, 0:N], [PWB[:, gi, :]], [DD[:, gi, 0:N]], reads=["PWB", "DD"], pskey=bk)
                act(MIX[:, 4 + gi, 0:N], bank[:, 0:N], AF.Identity, reads=[bk, "vecs"], writes=[("mix", 4 + gi)],
                    scale=vcol("pscale", gi), bias=0.0)
            for grp in range(2):
                (wv,), wk = wo[grp]
                for j in range(4):
                    oc = grp * 4 + j
                    bank, bk = newbank()
                    mmgroup(bank[:, 0:N], [wv[:, k, j * 128:(j + 1) * 128] for k in range(KC)], [MIX[:, k, 0:N] for k in range(KC)],
                            reads=[wk] + [("mix", k) for k in range(KC)], pskey=bk)
                    resid_update(bank, bk, oc, s, N, Gcol(0, 1, 0, oc), 8, ("GV", 0, 1))
            if e0 + N >= ET:
                apply_deferred(s + N, 8)

        St.nw = 2
        St.wslot = St.wslot % 2
        NS.sqoff = 14720
        NS.sqkeys = [("msp", 0), ("msp", 1), "sq2"]
        for (e0, N) in split_tiles(0, ET, tmax=256):
            mixer0_tile(e0, N)
        NS.sqoff = SQ_OFF
        NS.sqkeys = ["sq"]
        St.nw = 3

        fw.barrier()
        if stop_after == 2:
            finish_debug()
            return nc
        o1 = OWNOFF - 1
        BIGCAP = 1026
        St.xn = Kt[:, :, :].rearrange("p c n -> p (c n)")[:, 0:8 * BIGCAP].rearrange("p (c n) -> p c n", c=8)
        St.G = aview(0, 11 * BIGCAP, BF16).rearrange("p (c n) -> p c n", c=11)
        St.cap = BIGCAP
        NS.base = Vt[:, :, :].rearrange("p c n -> p (c n)")
        NS.cap = 2 * NPAIR * 512
        NS.sqoff = 0
        e_super = [[("x", o1, 512), ("x", o1 + 512, 342), ("x", o1 + 854, 171)],
                   [("x", o1 + 1025, 512), ("x", o1 + 1537, 342), ("x", o1 + 1879, 171)]]
        ffn(0, 1, e_super[0], nxt=(0, 1, e_super[1]))
        ffn(0, 1, e_super[1], nxt=(1, 0, e_super[0]) if stop_after is None else None)
        if stop_after == 3:
            finish_debug()
            return nc
        ffn(1, 0, e_super[0], nxt=(1, 0, e_super[1]))
        ffn(1, 0, e_super[1])

        fw.barrier()
        if stop_after == 4:
            finish_debug()
            return nc
        CG = aview(0, 2 * 456, F32).rearrange("p (i n) -> p i n", i=2)
        ZZ = aview(3648, 2 * 456, F32).rearrange("p (i n) -> p i n", i=2)
        YY = aview(7296, 2 * 456, F32).rearrange("p (i n) -> p i n", i=2)
        GZ = aview(10944, 8 * 448, BF16).rearrange("p (c n) -> p c n", c=8)

        own_tiles = split_tiles(OWNOFF, OWNT, tmax=448)
        for tix, (s, N) in enumerate(own_tiles):
            n2 = N + 2
            norm_mod(X[:, :, s - 1:s + N + 1], n2, xblocks(s - 1, n2), 1, 1, 0, xn[:, :, 0:n2], [("xn", 0)])
            if tix > 0:
                apply_deferred(s, 1)
            for c in range(KC):
                (wbg, wcg, wxi), wk = wload([cwin[:, c * 128:(c + 1) * 128], cwin[:, D + c * 128: D + (c + 1) * 128],
                                             cwin[:, 2 * D + c * 128: 2 * D + (c + 1) * 128]])
                pc, pck = newbank()
                px, pxk = newbank()
                pg, pgk = newbank()
                rhs2 = [xn[:, k, 0:n2] for k in range(KC)]
                mmgroup(pc[:, 0:n2], [wcg[:, k, :] for k in range(KC)], rhs2, reads=[wk, ("xn", 0)], pskey=pck)
                mmgroup(px[:, 0:n2], [wxi[:, k, :] for k in range(KC)], rhs2, reads=[wk, ("xn", 0)], pskey=pxk)
                mmgroup(pg[:, 0:N], [wbg[:, k, :] for k in range(KC)], [xn[:, k, 1:1 + N] for k in range(KC)], reads=[wk, ("xn", 0)], pskey=pgk)
                i = c % 2
                act(CG[:, i, 0:n2], pc[:, 0:n2], AF.Copy, reads=[pck], writes=[("cg", i)])
                dve_tt(ZZ[:, i, 0:n2], px[:, 0:n2], CG[:, i, 0:n2], ALU.mult, reads=[pxk, ("cg", i)], writes=[("zz", i)])
                if tix == 0:
                    dve_ts(ZZ[:, i, 0:1], ZZ[:, i, 0:1], vcol("vm", 0), None, ALU.mult, None, reads=[("zz", i), "vecs"], writes=[("zz", i)])
                if tix == len(own_tiles) - 1:
                    dve_ts(ZZ[:, i, n2 - 1:n2], ZZ[:, i, n2 - 1:n2], vcol("vm", 1), None, ALU.mult, None, reads=[("zz", i), "vecs"], writes=[("zz", i)])
                dve_ts(YY[:, i, 0:N], ZZ[:, i, 0:N], vcol("convw", 0 * 8 + c), None, ALU.mult, None, reads=[("zz", i), "vecs"], writes=[("yy", i)])
                dve_stt(YY[:, i, 0:N], ZZ[:, i, 1:1 + N], vcol("convw", 1 * 8 + c), YY[:, i, 0:N], ALU.mult, ALU.add,
                        reads=[("zz", i), ("yy", i), "vecs"], writes=[("yy", i)])
                dve_stt(YY[:, i, 0:N], ZZ[:, i, 2:2 + N], vcol("convw", 2 * 8 + c), YY[:, i, 0:N], ALU.mult, ALU.add,
                        reads=[("zz", i), ("yy", i), "vecs"], writes=[("yy", i)])
                dve_tt(GZ[:, c, 0:N], pg[:, 0:N], YY[:, i, 0:N], ALU.mult, reads=[pgk, ("yy", i)], writes=[("gz", c)])
            for grp in range(2):
                (wv,), wk = wload([cwout[:, grp * 512:(grp + 1) * 512]])
                for j in range(4):
                    oc = grp * 4 + j
                    bank, bk = newbank()
                    mmgroup(bank[:, 0:N], [wv[:, k, j * 128:(j + 1) * 128] for k in range(KC)], [GZ[:, k, 0:N] for k in range(KC)],
                            reads=[wk] + [("gz", k) for k in range(KC)], pskey=bk)
                    resid_update(bank, bk, oc, s, N, Gcol(1, 1, 0, oc), 1, ("GV", 1, 1))
            if tix == len(own_tiles) - 1:
                apply_deferred(s + N, 1)

        fw.barrier()
        if stop_after == 5:
            finish_debug()
            return nc
        f_super = [[("x", OWNOFF, 512), ("x", OWNOFF + 512, 512)], [("x", OWNOFF + 1024, 512), ("x", OWNOFF + 1536, 512)]]
        ffn(1, 1, f_super[0], nxt=(1, 1, f_super[1]))
        ffn(1, 1, f_super[1])

        fw.barrier()
        if stop_after == 6:
            finish_debug()
            return nc
        OS = aview(0, 8 * 512, F32).rearrange("p (c n) -> p c n", c=8)
        oTv = outT.rearrange("(c p) t -> p c t", p=128)
        for (s, N) in split_tiles(OWNOFF, OWNT, tmax=512):
            xk = xblocks(s, N)
            norm_stats(X[:, :, s:s + N], N, xk)
            for c in range(KC):
                i = St.td_i
                St.td_i ^= 1
                dve_tt(tdiv[:, i, 0:N], X[:, c, s:s + N], NS.rinv, ALU.mult, reads=xk + NS.sqkeys, writes=[("td", i)])
                act(OS[:, c, 0:N], tdiv[:, i, 0:N], AF.Identity, reads=[("td", i), "vecs"], writes=["OS"], scale=vcol("fg", c), bias=0.0)
            fw.dma("sp", "st", [(oTv[:, :, s - OWNOFF:s - OWNOFF + N], OS[:, :, 0:N])], reads=["OS"])

        fw.emit()
    return nc


def _fm(v):
    return np.ascontiguousarray(np.asarray(v, np.float32).reshape(-1, 128).T)


def _mask_col(core, m, p, a):
    out = np.zeros(128, np.float32)
    e = 2 * m + a
    gq = 32 * core - 1 + e
    for half in range(2):
        k = 2 * p + half
        gk = 32 * core - 5 + k
        if 0 <= gq < NROWS:
            rs = min(max(gq - 4, 0), NROWS - 8)
        else:
            rs = gq - 4
        valid = (0 <= gk < NROWS) and (rs <= gk < rs + 8)
        out[half * 64:(half + 1) * 64] = 0.0 if valid else NEG
    return out


def _build_vecs(core, norm_g, final_g, pool_scale, conv_w, mod_b):
    LAY = vec_layout()
    v = np.zeros((128, LAY["_n"]), np.float32)
    for l in range(2):
        for s in range(3):
            o = LAY["ng"] + (l * 3 + s) * 8
            v[:, o:o + 8] = _fm(norm_g[l, s])
    v[:, LAY["fg"]:LAY["fg"] + 8] = _fm(final_g)
    v[:, LAY["pscale"]:LAY["pscale"] + 4] = _fm(pool_scale[0])
    for tap in range(3):
        o = LAY["convw"] + tap * 8
        v[:, o:o + 8] = _fm(conv_w[0, tap])
    for l in range(2):
        o = LAY["modb"] + l * 72
        v[:, o:o + 72] = _fm(mod_b[l])
    v[:, LAY["vm"] + 0] = 0.0 if core == 0 else 1.0
    v[:, LAY["vm"] + 1] = 0.0 if core == NCORE - 1 else 1.0
    L = NROWS * GW
    for which in range(2):
        for gi, w in enumerate(POOL_W):
            for i in range(8):
                t = core * OWNT + (i if which == 0 else OWNT - 8 + i)
                lo = min(max(t - w // 2, 0), L)
                hi = min(max(t - w // 2 + w, 0), L)
                v[:, LAY["icnt"] + (which * 4 + gi) * 8 + i] = 1.0 / float(hi - lo)
    ref = None
    for c in range(NCORE):
        for m in range(3, 15):
            cols = [_mask_col(c, m, m + 4, 0), _mask_col(c, m, m + 4, 1), _mask_col(c, m, m, 0), _mask_col(c, m, m, 1)]
            cols = np.stack(cols, 1)
            if ref is None:
                ref = cols
            assert np.array_equal(ref, cols)
            for p in (m + 1, m + 2, m + 3):
                for a in range(2):
                    assert not _mask_col(c, m, p, a).any()
    v[:, LAY["mreg"]:LAY["mreg"] + 4] = ref
    for m in sorted(SPEC_UNITS):
        desc = unit_pairs_desc(m)
        for pi_, p in enumerate(desc):
            for a in range(2):
                v[:, LAY[("mspec", m)] + pi_ * 2 + a] = _mask_col(core, m, p, a)
    qc = np.arange(64)
    cs = np.clip(qc - 8, 0, 48)
    kc = np.arange(64)
    ok = (kc[:, None] >= cs[None, :]) & (kc[:, None] < cs[None, :] + 16)
    cm = np.where(ok, 0.0, NEG).astype(np.float32)
    v[:, LAY["cmask"]:LAY["cmask"] + 64] = np.concatenate([cm, cm], 0)
    return v


def _build_tb(rpb):
    tb = np.zeros((8, 128, 18, 64), np.float32)
    kc = np.arange(64)[:, None]
    qc = np.arange(64)[None, :]
    ci = np.clip(kc - qc + 15, 0, 30)
    for half in range(2):
        for jj in range(18):
            d = 8 - jj + half
            if -7 <= d <= 7:
                tb[:, half * 64:(half + 1) * 64, jj, :] = rpb[:, d + 7][:, ci]
    return np.ascontiguousarray(tb.reshape(8, 128, 1152))


_PROGRAM = None


def kernel(x, c, ctx, c_ctx, mod_w, mod_b, norm_g, ffn_w13, ffn_w2, even_w_in, even_w_out,
           na_rpb, pool_w, pool_scale, conv_w_in, conv_w, conv_w_out, final_g, _dbg=None):
    global _PROGRAM
    f = lambda a: np.ascontiguousarray(np.asarray(a, np.float32))
    x = f(x)[0]
    L = x.shape[0]
    ctxT = np.ascontiguousarray(f(ctx)[0].T)
    cc = np.stack([_fm(f(c)[0]), _fm(f(c_ctx))], -1).reshape(128, 16)
    tb = _build_tb(f(na_rpb)[0])
    shared = {
        "ctxT": ctxT, "cc": np.ascontiguousarray(cc), "tb": tb, "ident": np.eye(128, dtype=np.float32),
        "modw0": f(mod_w[0]), "modw1": f(mod_w[1]),
        "win": f(even_w_in[0]), "wout": f(even_w_out[0]),
        "poolw": np.ascontiguousarray(f(pool_w[0]).reshape(512, 128)),
        "cwin": f(conv_w_in[0]), "cwout": f(conv_w_out[0]),
    }
    for l in range(2):
        for ff in range(2):
            shared["w13_%d%d" % (l, ff)] = f(ffn_w13[l, ff])
            shared["w2_%d%d" % (l, ff)] = f(ffn_w2[l, ff])
    in_maps = []
    for core in range(NCORE):
        t_lo = core * OWNT - 5 * GW
        xe = np.zeros((KVT, D), np.float32)
        a = max(t_lo, 0)
        b = min(t_lo + KVT, L)
        xe[a - t_lo:b - t_lo] = x[a:b]
        m = dict(shared)
        m["xT"] = np.ascontiguousarray(xe.T)
        m["vecs"] = _build_vecs(core, f(norm_g), f(final_g), f(pool_scale), f(conv_w), f(mod_b))
        in_maps.append(m)
    if _dbg is not None:
        stop_after, cores = _dbg
        prog = build_program(stop_after=stop_after)
        res = run_bass_kernel_spmd(prog, [in_maps[i] for i in cores], core_ids=list(range(len(cores))))
        return res.results
    if _PROGRAM is None:
        _PROGRAM = build_program()
    res = run_bass_kernel_spmd(_PROGRAM, in_maps, core_ids=list(range(NCORE)))
    outs = [np.asarray(r["outT"]).T for r in res.results]
    return np.ascontiguousarray(np.concatenate(outs, 0)[None].astype(np.float32))
```
